# Optimizing a Trainium2 kernel written in Bass

```python
import jax, jax.numpy as jnp
from jax import lax
import numpy as np

D_MODEL = 1024
BATCH = 2
SEQ = 8192
DEPTH = 1
DEC_BATCH = 128
DEC_SEQ = 4
PAST_LEN = 8192
PAGE_SIZE = 128

HEAD_DIM = 64
CONV_W = D_MODEL // 4
ATTN_W = D_MODEL // 2
MEM_W = D_MODEL // 4
ATTN_HEADS = ATTN_W // HEAD_DIM
MEM_HEADS = MEM_W // HEAD_DIM
CONV_GROUPS = CONV_W // HEAD_DIM
D_MIX = CONV_W + ATTN_W + MEM_W
CONV_WIDTH = 3
N_MEM = 256
DILATED_CONFIGS = ((128, 1), (512, 4), (2048, 16))
WINDOW_MAX = 2048
ATTN_BLOCK = 128
ALIBI_MAX = 8.0
RMS_EPS = 1e-6
SPLITS = (CONV_W, CONV_W, CONV_W, ATTN_W, ATTN_W, ATTN_W, MEM_W, D_MIX)
D_IN = sum(SPLITS)

kernel_name = "hymba_conv_dilated_alibi_memxattn_step"

F32 = jnp.float32


def _rmsnorm(x, g):
    xf = x.astype(F32)
    r = lax.rsqrt(jnp.mean(xf * xf, axis=-1, keepdims=True) + RMS_EPS)
    return (xf * r * g.astype(F32)).astype(x.dtype)


def _alibi_slopes():
    h = jnp.arange(ATTN_HEADS, dtype=F32) + 1.0
    return jnp.exp2(-ALIBI_MAX * h / ATTN_HEADS)


def _project(x, g_in, w_in):
    h = _rmsnorm(x, g_in)
    p = h @ w_in
    idx = np.cumsum(SPLITS)[:-1].tolist()
    cb, cc, ch, q, k, v, mq, z = jnp.split(p, idx, axis=-1)
    lead = x.shape[:2]
    q = q.reshape(lead + (ATTN_HEADS, HEAD_DIM))
    k = k.reshape(lead + (ATTN_HEADS, HEAD_DIM))
    v = v.reshape(lead + (ATTN_HEADS, HEAD_DIM))
    mq = mq.reshape(lead + (MEM_HEADS, HEAD_DIM))
    return cb, cc, ch, q, k, v, mq, z


def _conv3(full, w):
    return w[0] * full[:, :-2] + w[1] * full[:, 1:-1] + w[2] * full[:, 2:]


def _dilated_band(q, k, v, win, dil, slopes):
    B, T, H, E = q.shape
    span = win // dil
    L = T // dil
    nb = -(-L // ATTN_BLOCK)
    Lp = nb * ATTN_BLOCK

    def fold(a):
        a = a.reshape(B, L, dil, H, E).transpose(0, 2, 1, 3, 4)
        a = jnp.pad(a, ((0, 0), (0, 0), (0, Lp - L), (0, 0), (0, 0)))
        return a.reshape(B, dil, nb, ATTN_BLOCK, H, E)

    def with_prev(a):
        prev = jnp.pad(a, ((0, 0), (0, 0), (1, 0), (0, 0), (0, 0), (0, 0)))[:, :, :-1]
        return jnp.concatenate([prev, a], axis=3)

    qb = fold(q)
    kk = with_prev(fold(k))
    vv = with_prev(fold(v))
    s = jnp.einsum('brnqhe,brnkhe->brnhqk', qb, kk, preferred_element_type=F32) * (HEAD_DIM ** -0.5)
    qi = jnp.arange(ATTN_BLOCK)[:, None] + ATTN_BLOCK
    ki = jnp.arange(2 * ATTN_BLOCK)[None, :]
    dist = qi - ki
    kglob = jnp.arange(nb)[:, None, None] * ATTN_BLOCK + ki - ATTN_BLOCK
    valid = (dist >= 0) & (dist <= span) & (kglob >= 0)
    bias = -(slopes * dil)[:, None, None] * dist
    s = jnp.where(valid[:, None], s + bias, -jnp.inf)
    m = jnp.max(s, axis=-1, keepdims=True)
    p = jnp.exp(s - m)
    l = jnp.sum(p, axis=-1, keepdims=True)
    o = jnp.einsum('brnhqk,brnkhe->brnqhe', p, vv.astype(F32))
    l_t = jnp.transpose(l[..., 0], (0, 1, 2, 4, 3))[..., None]
    o = o / l_t
    lse = jnp.transpose((m + jnp.log(l))[..., 0], (0, 1, 2, 4, 3))
    o = o.reshape(B, dil, Lp, H, E)[:, :, :L].transpose(0, 2, 1, 3, 4).reshape(B, T, H, E)
    lse = lse.reshape(B, dil, Lp, H)[:, :, :L].transpose(0, 2, 1, 3).reshape(B, T, H)
    return o, lse


def _dilated_gather(q, kfull, vfull, win, dil, slopes, n_buf):
    S = q.shape[1]
    span = win // dil
    j = jnp.arange(S)[:, None]
    kd = jnp.arange(span + 1)[None, :]
    idx = n_buf + j - kd * dil
    valid = idx >= 0
    idx = jnp.maximum(idx, 0)
    kg = kfull[:, idx]
    vg = vfull[:, idx]
    s = jnp.einsum('bshe,bskhe->bhsk', q, kg, preferred_element_type=F32) * (HEAD_DIM ** -0.5)
    s = s - (slopes * dil)[:, None, None] * kd
    s = jnp.where(valid, s, -jnp.inf)
    m = jnp.max(s, axis=-1, keepdims=True)
    p = jnp.exp(s - m)
    l = jnp.sum(p, axis=-1, keepdims=True)
    o = jnp.einsum('bhsk,bskhe->bshe', p, vg.astype(F32))
    o = o / jnp.transpose(l[..., 0], (0, 2, 1))[..., None]
    lse = jnp.transpose((m + jnp.log(l))[..., 0], (0, 2, 1))
    return o, lse


def _merge_by_denominator(outs, lses):
    o = jnp.stack(outs, 0)
    a = jax.nn.softmax(jnp.stack(lses, 0), axis=0)
    return jnp.sum(a[..., None] * o, axis=0)


def _mem_attn(qm, mk, mv):
    s = jnp.einsum('bthe,bmhe->bhtm', qm, mk, preferred_element_type=F32) * (HEAD_DIM ** -0.5)
    p = jax.nn.softmax(s, axis=-1)
    return jnp.einsum('bhtm,bmhe->bthe', p, mv.astype(F32))


def _mem_kv(mem, g_mem, w_mem_kv):
    kv = _rmsnorm(mem, g_mem) @ w_mem_kv
    mk, mv = jnp.split(kv, 2, axis=-1)
    lead = mem.shape[:2]
    return mk.reshape(lead + (MEM_HEADS, HEAD_DIM)), mv.reshape(lead + (MEM_HEADS, HEAD_DIM))


def _finish(x, conv_y, attn_o, mem_o, z, w_out):
    lead = x.shape[:2]
    mix = jnp.concatenate([conv_y,
                           attn_o.reshape(lead + (ATTN_W,)).astype(x.dtype),
                           mem_o.reshape(lead + (MEM_W,)).astype(x.dtype)], axis=-1)
    mix = mix * jax.nn.silu(z)
    return x + mix @ w_out


def setup_inputs(seed: int = 0) -> dict:
    key = jax.random.key(seed)
    ks = jax.random.split(key, 20)
    n_buf = min(WINDOW_MAX, PAST_LEN)
    nrm = jax.random.normal
    return {
        "x_prompt": nrm(ks[0], (BATCH, SEQ, D_MODEL), F32),
        "x_sample": nrm(ks[1], (DEC_BATCH, DEC_SEQ, D_MODEL), F32),
        "mem_prompt": nrm(ks[2], (BATCH, N_MEM, D_MODEL), F32),
        "cache_win_k": nrm(ks[3], (DEPTH, DEC_BATCH, n_buf, ATTN_HEADS, HEAD_DIM), F32),
        "cache_win_v": nrm(ks[4], (DEPTH, DEC_BATCH, n_buf, ATTN_HEADS, HEAD_DIM), F32),
        "cache_conv": nrm(ks[5], (DEPTH, DEC_BATCH, CONV_WIDTH - 1, CONV_W), F32),
        "cache_mem_k": nrm(ks[6], (DEPTH, DEC_BATCH, N_MEM, MEM_HEADS, HEAD_DIM), F32),
        "cache_mem_v": nrm(ks[7], (DEPTH, DEC_BATCH, N_MEM, MEM_HEADS, HEAD_DIM), F32),
        "g_in": 1.0 + 0.02 * nrm(ks[8], (DEPTH, D_MODEL), F32),
        "w_in": nrm(ks[9], (DEPTH, D_MODEL, D_IN), F32) * D_MODEL ** -0.5,
        "conv_w": nrm(ks[10], (DEPTH, CONV_WIDTH, CONV_W), F32) * CONV_WIDTH ** -0.5,
        "g_mem": 1.0 + 0.02 * nrm(ks[11], (DEPTH, D_MODEL), F32),
        "w_mem_kv": nrm(ks[12], (DEPTH, D_MODEL, 2 * MEM_W), F32) * D_MODEL ** -0.5,
        "w_out": nrm(ks[13], (DEPTH, D_MIX, D_MODEL), F32) * D_MIX ** -0.5,
        "g_final": 1.0 + 0.02 * nrm(ks[14], (D_MODEL,), F32),
    }


def reference(x_prompt, x_sample, mem_prompt, cache_win_k, cache_win_v, cache_conv,
              cache_mem_k, cache_mem_v, g_in, w_in, conv_w, g_mem, w_mem_kv, w_out, g_final):
    slopes = _alibi_slopes()
    xp, xs = x_prompt, x_sample
    n_buf = cache_win_k.shape[2]
    p_wk, p_wv, p_cv, p_mk, p_mv, s_wk, s_wv, s_cv = [], [], [], [], [], [], [], []
    for l in range(DEPTH):
        cb, cc, ch, q, k, v, mq, z = _project(xp, g_in[l], w_in[l])
        u = cc * ch
        up = jnp.pad(u, ((0, 0), (CONV_WIDTH - 1, 0), (0, 0)))
        conv_y = cb * _conv3(up, conv_w[l])
        outs, lses = [], []
        for win, dil in DILATED_CONFIGS:
            o, lse = _dilated_band(q, k, v, win, dil, slopes)
            outs.append(o)
            lses.append(lse)
        attn_o = _merge_by_denominator(outs, lses)
        mk, mv = _mem_kv(mem_prompt, g_mem[l], w_mem_kv[l])
        mem_o = _mem_attn(mq, mk, mv)
        n_keep = min(WINDOW_MAX, xp.shape[1])
        p_wk.append(k[:, -n_keep:])
        p_wv.append(v[:, -n_keep:])
        p_cv.append(u[:, -(CONV_WIDTH - 1):])
        p_mk.append(mk)
        p_mv.append(mv)
        xp = _finish(xp, conv_y, attn_o, mem_o, z, w_out[l])

        cb, cc, ch, q, k, v, mq, z = _project(xs, g_in[l], w_in[l])
        u = cc * ch
        full = jnp.concatenate([cache_conv[l].astype(u.dtype), u], axis=1)
        conv_y = cb * _conv3(full, conv_w[l])
        kfull = jnp.concatenate([cache_win_k[l].astype(k.dtype), k], axis=1)
        vfull = jnp.concatenate([cache_win_v[l].astype(v.dtype), v], axis=1)
        outs, lses = [], []
        for win, dil in DILATED_CONFIGS:
            o, lse = _dilated_gather(q, kfull, vfull, win, dil, slopes, n_buf)
            outs.append(o)
            lses.append(lse)
        attn_o = _merge_by_denominator(outs, lses)
        mem_o = _mem_attn(mq, cache_mem_k[l].astype(mq.dtype), cache_mem_v[l].astype(mq.dtype))
        s_wk.append(k)
        s_wv.append(v)
        s_cv.append(full[:, -(CONV_WIDTH - 1):])
        xs = _finish(xs, conv_y, attn_o, mem_o, z, w_out[l])

    y_prompt = _rmsnorm(xp, g_final)
    y_sample = _rmsnorm(xs, g_final)
    return (y_prompt, y_sample,
            jnp.stack(p_wk, 0), jnp.stack(p_wv, 0), jnp.stack(p_cv, 0),
            jnp.stack(p_mk, 0), jnp.stack(p_mv, 0),
            jnp.stack(s_wk, 0), jnp.stack(s_wv, 0), jnp.stack(s_cv, 0))
```

```python
import contextlib
import numpy as np
import ml_dtypes
import concourse.bass as bass
import concourse.mybir as mybir
from concourse.bass_utils import run_bass_kernel_spmd

F32 = mybir.dt.float32
BF16 = mybir.dt.bfloat16
AF = mybir.ActivationFunctionType
ALU = mybir.AluOpType
AX = mybir.AxisListType

NCORES = 8
D = 1024
T = 2048
TH = 2048
DIN = 3584
NS = 64
NB = 16
EPS = 1e-6
DILS = (1, 4, 16)
NEG = -30000.0

STAGE = 9


class Buf:
    __slots__ = ("name", "writers", "readers", "dsem", "dcnt")

    def __init__(self, name):
        self.name = name
        self.writers = []
        self.readers = []
        self.dsem = None
        self.dcnt = 0


class Prog:
    ENGS = ("pe", "act", "dve", "pool", "sp")

    def __init__(self, nc):
        self.nc = nc
        self.ops = []
        self.last = {}
        self.dma_last = {}
        self.out_dmas = []

    def _prune(self, lst, i):
        r = self.ops[i]
        out = []
        for j in lst:
            o = self.ops[j]
            if r["dma"] is None and o["dma"] is None and o["eng"] == r["eng"]:
                continue
            if r["dma"] is not None and o["dma"] is r["dma"]:
                continue
            out.append(j)
        out.append(i)
        return out

    def op(self, eng, fn, reads=(), writes=(), dma=None, deps=(), out=False):
        i = len(self.ops)
        d = set(deps)
        for b in reads:
            d.update(b.writers)
        for b in writes:
            for w in b.writers:
                if dma is not None and self.ops[w]["dma"] is dma:
                    continue
                d.add(w)
            d.update(b.readers)
        self.ops.append(dict(eng=eng, fn=fn, deps=sorted(d), dma=dma, sig=False, tok=None))
        for b in writes:
            if b.readers:
                b.writers = [i]
                b.readers = []
            else:
                b.writers = self._prune(b.writers, i)
        for b in reads:
            b.readers = self._prune(b.readers, i)
        if dma is None:
            self.last[eng] = i
        else:
            self.dma_last[dma.name] = i
            if out:
                self.out_dmas.append(i)
        return i

    def barrier(self, scratch):
        deps = list(self.last.values()) + list(self.dma_last.values())
        ids = []
        for k, eng in enumerate(("act", "dve", "pool")):
            ap = scratch[0:1, k:k + 1]
            if eng == "act":
                ap2 = scratch[0:1, 4:5]
                ids.append(self.op(eng, (lambda a, a2: (lambda e: e.activation(out=a, in_=a2, func=AF.Copy)))(ap, ap2), deps=deps))
            else:
                ids.append(self.op(eng, (lambda a: (lambda e: e.memset(a, 0.0)))(ap), deps=deps))
        return ids

    def emit(self, final_eng="sp"):
        nc = self.nc
        ops = self.ops
        for r in ops:
            for dd in r["deps"]:
                ops[dd]["sig"] = True
        esem = {e: nc.alloc_semaphore("sem_" + e) for e in self.ENGS}
        cnt = {e: 0 for e in self.ENGS}
        for r in ops:
            if r["dma"] is not None:
                b = r["dma"]
                if b.dsem is None:
                    b.dsem = nc.alloc_semaphore("dsem_" + b.name)
                b.dcnt += 1
                r["tok"] = (b.dsem, 16 * b.dcnt)
            elif r["sig"]:
                cnt[r["eng"]] += 1
                r["tok"] = (esem[r["eng"]], cnt[r["eng"]])
        outs = list(self.out_dmas)

        def run(engname):
            def f(eng):
                waited = {}

                def wait_tok(tok):
                    sem, val = tok
                    if waited.get(sem.num, 0) >= val:
                        return
                    eng.wait_ge(sem, val)
                    waited[sem.num] = val

                for r in ops:
                    if r["eng"] != engname:
                        continue
                    need = {}
                    for dd in r["deps"]:
                        o = ops[dd]
                        if engname == "pe" and o["eng"] == "pe" and o["dma"] is None:
                            continue
                        sem, val = o["tok"]
                        if need.get(sem.num, (None, 0))[1] < val:
                            need[sem.num] = (sem, val)
                    for sem, val in need.values():
                        wait_tok((sem, val))
                    ins = r["fn"](eng)
                    if r["tok"] is not None:
                        ins.then_inc(r["tok"][0], 16 if r["dma"] is not None else 1)
                if engname == final_eng:
                    for i in outs:
                        wait_tok(ops[i]["tok"])
            return f

        with nc.Block() as block:
            block.sync(run("sp"))
            block.scalar(run("act"))
            block.vector(run("dve"))
            block.gpsimd(run("pool"))
            block.tensor(run("pe"))


def sub_ap(base, extra):
    return bass.AP(tensor=base.tensor, offset=base.offset, ap=[list(base.ap[0])] + [list(x) for x in extra])


def build_program():
    nc = bass.Bass("TRN2", target_bir_lowering=False)
    P = Prog(nc)

    def din(name, shape, dt=F32):
        return nc.dram_tensor(name, list(shape), dt, kind="ExternalInput").ap()

    def dout(name, shape, dt=F32):
        return nc.dram_tensor(name, list(shape), dt, kind="ExternalOutput").ap()

    xo = din("xo", [T, D]); xh = din("xh", [TH, D]); mem = din("mem", [256, D]); xs_in = din("xs", [NS, D])
    cwk = din("cwk", [NB, 2048, 512]); cwv = din("cwv", [NB, 2048, 512])
    ccv = din("ccv", [NB * 2, 256]); cmk = din("cmk", [NB, 256, 256]); cmv = din("cmv", [NB, 256, 256])
    w_in = din("w_in", [D, DIN]); w_mem = din("w_mem", [D, 512]); w_out = din("w_out", [D, D])
    gin_d = din("gin", [128, D]); gmem_d = din("gmem", [128, D]); gfin_d = din("gfin", [128, D])
    convw_d = din("convw", [128, 2, 3]); ident_d = din("ident", [128, 128]); hflag_d = din("hflag", [128, 128])
    btab_d = din("btab", [4, 128, 3 * 2 * 512])
    wmain_d = din("wmain", [32, 1024]); wnew_d = din("wnew", [32, NB * 64])
    bmask_d = din("bmask", [32, 512]); bmaskm_d = din("bmaskm", [16, 256]); par_d = din("par", [32, 2])

    y_d = dout("y", [T, D]); ys_d = dout("ys", [NS, D])
    pwk_d = dout("pwk", [T, 512]); pwv_d = dout("pwv", [T, 512]); pconv_d = dout("pconv", [2, 256])
    pmk_d = dout("pmk", [256, 256]); pmv_d = dout("pmv", [256, 256])
    swk_d = dout("swk", [NS, 512]); swv_d = dout("swv", [NS, 512]); sconv_d = dout("sconv", [NB, 2, 256])
    vs_d = nc.dram_tensor("vscratch", [4, TH + T, 128], BF16).ap()

    es = contextlib.ExitStack()
    bufs = {}

    stack_bufs = {}
    freed = set()

    def sb(name, shape, dt, stack=None):
        t = (stack or es).enter_context(nc.sbuf_tensor("s_" + name, list(shape), dt))
        bufs[name] = Buf(name)
        stack_bufs.setdefault(id(stack or es), []).append(bufs[name])
        return t, bufs[name]

    def collect(stack, extra=()):
        for b in stack_bufs.get(id(stack), []) + list(extra):
            freed.update(b.writers)
            freed.update(b.readers)

    with es:
        banks = []
        for i in range(8):
            t = es.enter_context(nc.psum_tensor("bank%d" % i, [128, 512], F32))
            banks.append((t, Buf("bank%d" % i)))
        bank_rr = {"main": 0, "aux": 0}
        bank_pool = {"main": list(range(8)), "aux": [6, 7]}

        def next_bank(pool="main"):
            lst = bank_pool[pool]
            b = banks[lst[bank_rr[pool] % len(lst)]]
            bank_rr[pool] += 1
            return b

        def bf(bank_t):
            return bank_t[:].bitcast(BF16)

        ident_f, bidf = sb("ident_f", [128, 128], F32)
        ident_b, bidb = sb("ident_b", [128, 128], BF16)
        ones_b, bones = sb("ones_b", [128, 128], BF16)
        hones_f, bhof = sb("hones_f", [128, 128], F32)
        hones_b, bhob = sb("hones_b", [128, 128], BF16)
        scr, bscr = sb("scr", [128, 8], F32)
        epsb, bepsb = sb("epsb", [128, 1], F32)
        mixA, bmixA = sb("mixA", [128, 4, T], BF16)
        mixB, bmixB = sb("mixB", [128, 4, T], BF16)
        mkT, bmkT = sb("mkT", [128, 2, 256], BF16)
        mvb, bmvb = sb("mvb", [128, 2, 256], BF16)
        mixS, bmixS = sb("mixS", [128, 8, NS], BF16)
        stats, _bstats0 = sb("stats", [128, 8], F32)
        bstatc = [Buf("stats%d" % i) for i in range(8)]
        hS, bhS = sb("hS", [128, 8, NS], BF16)
        qS, bqS = sb("qS", [128, 4, NS], BF16)
        kS, bkS = sb("kS", [128, 4, NS], BF16)
        mqS, bmqS = sb("mqS", [128, 2, NS], BF16)
        szS, bszS = sb("szS", [128, 8, NS], F32)
        vSb, bvSb = sb("vSb", [NS, 512], BF16)
        uS, buS = sb("uS", [128, 2, NB, 6], F32)
        cbS, bcbS = sb("cbS", [128, 2, NS], F32)
        pk = contextlib.ExitStack()
        kT, bkT = sb("kT", [128, 4, TH + T], BF16, pk)
        qT, bqT = sb("qT", [128, 4, T], BF16, pk)
        szA, bszA = sb("szA", [128, 4, T], BF16, pk)
        gin, bgin = sb("gin_t", [128, D], F32, pk)

        c0 = P.op("sp", lambda e: e.dma_start(out=gin[:], in_=gin_d), writes=[bgin], dma=bgin)
        P.op("sp", lambda e: e.dma_start(out=ident_f[:], in_=ident_d), writes=[bidf], dma=bidf)
        P.op("sp", lambda e: e.dma_start(out=hones_f[:], in_=hflag_d), writes=[bhof], dma=bhof)
        P.op("dve", lambda e: e.tensor_copy(out=ident_b[:], in_=ident_f[:]), reads=[bidf], writes=[bidb])
        P.op("dve", lambda e: e.tensor_copy(out=hones_b[:], in_=hones_f[:]), reads=[bhof], writes=[bhob])
        P.op("pool", lambda e: e.memset(ones_b[:], 1.0), writes=[bones])
        P.op("pool", lambda e: e.memset(epsb[:], EPS), writes=[bepsb])

        def rms_scale(x_t, bx, n, g_t, bg, xs_t, bxs, col):
            bstats = bstatc[col]
            P.op("act", lambda e: e.activation(out=xs_t[0:n, :], in_=x_t[0:n, :], func=AF.Square,
                                               accum_out=stats[0:n, col:col + 1]),
                 reads=[bx], writes=[bstats, bxs])
            P.op("act", lambda e: e.activation(out=stats[0:n, col:col + 1], in_=stats[0:n, col:col + 1], func=AF.Ln,
                                               scale=1.0 / D, bias=epsb[0:n, 0:1]),
                 reads=[bstats, bepsb], writes=[bstats])
            P.op("act", lambda e: e.activation(out=stats[0:n, col:col + 1], in_=stats[0:n, col:col + 1], func=AF.Exp, scale=-0.5),
                 reads=[bstats], writes=[bstats])
            P.op("dve", lambda e: e.scalar_tensor_tensor(out=xs_t[0:n, :], in0=x_t[0:n, :],
                                                         scalar=stats[0:n, col:col + 1], in1=g_t[0:n, :],
                                                         op0=ALU.mult, op1=ALU.mult),
                 reads=[bx, bstats, bg], writes=[bxs])

        def transpose_to(xs_t, bxs, n, dst_fn, bdst, evac_eng):
            bt, bb = next_bank()
            v = bf(bt).rearrange("p (k t) -> p k t", t=128)
            for kc in range(8):
                P.op("pe", (lambda kc: lambda e: e.transpose(v[:, kc, 0:n], xs_t[0:n, kc * 128:(kc + 1) * 128],
                                                              ident_b[0:n, 0:n]))(kc),
                     reads=[bxs, bidb], writes=[bb])
            dst = dst_fn()
            if evac_eng == "act":
                P.op("act", lambda e: e.activation(out=dst, in_=v[:, :, 0:n], func=AF.Copy), writes=[bb, bdst])
            else:
                P.op("dve", lambda e: e.tensor_copy(out=dst, in_=v[:, :, 0:n]), writes=[bb, bdst])

        def proj_fm(W_t, bW, col0, hT_ap, bh, n):
            bt, bb = next_bank()
            for kc in range(8):
                P.op("pe", (lambda kc: lambda e: e.matmul(bt[:, 0:n], lhsT=W_t[:, kc, col0:col0 + 128],
                                                           rhs=hT_ap(kc), start=(kc == 0), stop=(kc == 7)))(kc),
                     reads=[bW, bh], writes=[bb])
            return bt, bb

        def proj_tm(W_t, bW, col0, ncol, hT_ap, bh, ntok=128):
            bt, bb = next_bank()
            for kc in range(8):
                P.op("pe", (lambda kc: lambda e: e.matmul(bt[0:ntok, 0:ncol], lhsT=hT_ap(kc),
                                                           rhs=W_t[:, kc, col0:col0 + ncol],
                                                           start=(kc == 0), stop=(kc == 7)))(kc),
                     reads=[bW, bh], writes=[bb])
            return bt, bb

        pAB = contextlib.ExitStack()
        NXB = 2
        xt = [sb("xt%d" % i, [128, D], F32, pAB) for i in range(NXB)]
        xsb = [sb("xsb%d" % i, [128, D], BF16, pAB) for i in range(4)]
        hTb = [sb("hT%d" % i, [128, 8, 512], BF16, pAB) for i in range(2)]
        pr = contextlib.ExitStack()
        WrA = pr.enter_context(nc.sbuf_tensor("s_WrA", [128, 8, 1024], BF16, side="right"))
        bWrA = Buf("WrA")
        ph = contextlib.ExitStack()
        with ph:
            Wi, bWi = sb("Wkv", [128, 8, 1024], BF16, ph)
            Wm, bWm = sb("Wm", [128, 8, 512], BF16, ph)
            gmem, bgmem = sb("gmem_t", [128, D], F32, ph)
            P.op("pool", lambda e: e.dma_start(out=Wm[:], in_=w_mem.rearrange("(k p) n -> p k n", p=128)),
                 writes=[bWm], dma=bWm)
            for kc in range(8):
                P.op("pool", (lambda kc, W: lambda e: e.dma_start(out=W[:, kc, :], in_=w_in[kc * 128:(kc + 1) * 128, 1280:2304]))(kc, Wi),
                     writes=[bWi], dma=bWi)
            P.op("sp", lambda e: e.dma_start(out=gmem[:], in_=gmem_d), writes=[bgmem], dma=bgmem)

            def prefetch_wra(kcs):
                for kc in kcs:
                    P.op("pool", (lambda kc: lambda e: e.dma_start(out=WrA[:, kc, 0:512], in_=w_in[kc * 128:(kc + 1) * 128, 768:1280]))(kc),
                         writes=[bWrA], dma=bWrA)
                    P.op("pool", (lambda kc: lambda e: e.dma_start(out=WrA[:, kc, 512:1024], in_=w_in[kc * 128:(kc + 1) * 128, 2816:3328]))(kc),
                         writes=[bWrA], dma=bWrA)

            kvst = [sb("kvst%d" % i, [128, 1024], F32, ph) for i in range(2)]
            vbst = [sb("vbst%d" % i, [128, 512], BF16, ph) for i in range(2)]
            bvs = Buf("vscratch")
            cnt = {"x": 0, "xs": 0, "kv": 0, "vb": 0, "pm": 0}

            for mt in range(2):
                x_t, bx = xt[cnt["x"] % NXB]; cnt["x"] += 1
                P.op("sp", (lambda mt, x_t: lambda e: e.dma_start(out=x_t[:], in_=mem[mt * 128:(mt + 1) * 128, :]))(mt, x_t),
                     writes=[bx], dma=bx)
                s_t, bs = xsb[cnt["xs"] % len(xsb)]; cnt["xs"] += 1
                rms_scale(x_t, bx, 128, gmem, bgmem, s_t, bs, 0)
                h_t, bh = hTb[0]
                transpose_to(s_t, bs, 128, (lambda mt, h_t: lambda: h_t[:, :, mt * 128:(mt + 1) * 128])(mt, h_t), bh, "act")
            h_t, bh = hTb[0]
            for cm in range(2):
                bt, bb = proj_fm(Wm, bWm, cm * 128, (lambda h_t: lambda kc: h_t[:, kc, 0:256])(h_t), bh, 256)
                P.op("act", (lambda cm, bt: lambda e: e.activation(out=mkT[:, cm, :], in_=bt[:, 0:256], func=AF.Copy))(cm, bt),
                     writes=[bb, bmkT])
            for mt in range(2):
                bt, bb = proj_tm(Wm, bWm, 0, 512, (lambda mt, h_t: lambda kc: h_t[:, kc, mt * 128:(mt + 1) * 128])(mt, h_t), bh)
                st, bst = kvst[cnt["kv"] % 2]; cnt["kv"] += 1
                P.op("dve", (lambda st, bt: lambda e: e.tensor_copy(out=st[:, 0:512], in_=bt[:, 0:512]))(st, bt),
                     writes=[bb, bst])
                P.op("pool", (lambda st, mt: lambda e: e.tensor_copy(out=mvb[:, mt, :], in_=st[:, 256:512]))(st, mt),
                     reads=[bst], writes=[bmvb])
                P.op("pool", (lambda st, mt: lambda e: e.dma_start(out=pmk_d[mt * 128:(mt + 1) * 128, :], in_=st[:, 0:256]))(st, mt),
                     reads=[bst], dma=bst, out=True)
                P.op("pool", (lambda st, mt: lambda e: e.dma_start(out=pmv_d[mt * 128:(mt + 1) * 128, :], in_=st[:, 256:512]))(st, mt),
                     reads=[bst], dma=bst, out=True)

            def norm_part1(src, blk):
                res = []
                for tl in range(4):
                    x_t, bx = xt[cnt["x"] % NXB]; cnt["x"] += 1
                    r0 = blk * 512 + tl * 128
                    P.op("sp", (lambda x_t, r0: lambda e: e.dma_start(out=x_t[:], in_=src[r0:r0 + 128, :]))(x_t, r0),
                         writes=[bx], dma=bx)
                    s_t, bs = xsb[cnt["xs"] % len(xsb)]; cnt["xs"] += 1
                    rms_scale(x_t, bx, 128, gin, bgin, s_t, bs, 1 + (tl % 2))
                    res.append((s_t, bs))
                return res

            def norm_part2(res, h_t, bh):
                for tl, (s_t, bs) in enumerate(res):
                    transpose_to(s_t, bs, 128, (lambda tl, h_t: lambda: h_t[:, :, tl * 128:(tl + 1) * 128])(tl, h_t), bh,
                                 "act" if tl % 2 == 0 else "dve")

            def v_tokmajor(h_t, bh, tok0, own):
                for tl in range(4):
                    hap = (lambda tl: lambda kc: h_t[:, kc, tl * 128:(tl + 1) * 128])(tl)
                    vb, bvb = vbst[cnt["vb"] % 2]; cnt["vb"] += 1
                    r0 = tok0 + tl * 128
                    if own:
                        st, bst = kvst[cnt["kv"] % 2]; cnt["kv"] += 1
                        bt, bb = proj_tm(Wi, bWi, 0, 512, hap, bh)
                        P.op("act", (lambda st, bt: lambda e: e.activation(out=st[:, 0:512], in_=bt[:, 0:512], func=AF.Copy))(st, bt),
                             writes=[bb, bst])
                    bt2, bb2 = proj_tm(Wi, bWi, 512, 512, hap, bh)
                    P.op("dve", (lambda vb, bt2: lambda e: e.tensor_copy(out=vb[:], in_=bt2[:, 0:512]))(vb, bt2),
                         writes=[bb2, bvb])
                    if own:
                        P.op("act", (lambda st, bt2: lambda e: e.activation(out=st[:, 512:1024], in_=bt2[:, 0:512], func=AF.Copy))(st, bt2),
                             writes=[bb2, bst])
                    P.op("pool", (lambda vb, r0: lambda e: e.dma_start(
                        out=vs_d.rearrange("q t c -> t q c")[r0:r0 + 128],
                        in_=vb[:].rearrange("p (q c) -> p q c", c=128)))(vb, r0),
                         reads=[bvb], writes=[bvs], dma=bvb)
                    if own:
                        o0 = r0 - TH
                        P.op("pool", (lambda st, o0: lambda e: e.dma_start(out=pwk_d[o0:o0 + 128, :], in_=st[:, 0:512]))(st, o0),
                             reads=[bst], dma=bst, out=True)
                        P.op("pool", (lambda st, o0: lambda e: e.dma_start(out=pwv_d[o0:o0 + 128, :], in_=st[:, 512:1024]))(st, o0),
                             reads=[bst], dma=bst, out=True)

            norm_part2(norm_part1(xh, 0), hTb[0][0], hTb[0][1])
            for blk in range(8):
                own = blk >= 4
                h_t, bh = hTb[blk % 2]
                nres = None
                if blk + 1 < 8:
                    nres = norm_part1(xo if blk + 1 >= 4 else xh, (blk + 1) % 4)
                tok0 = blk * 512
                if 2 <= blk <= 5:
                    prefetch_wra([2 * (blk - 2), 2 * (blk - 2) + 1])
                hap512 = (lambda h_t: lambda kc: h_t[:, kc, :])(h_t)
                for c in range(4):
                    bt, bb = proj_fm(Wi, bWi, c * 128, hap512, bh, 512)
                    P.op("act", (lambda c, bt, tok0: lambda e: e.activation(out=kT[:, c, tok0:tok0 + 512], in_=bt[:, :], func=AF.Copy))(c, bt, tok0),
                         writes=[bb, bkT])
                v_tokmajor(h_t, bh, tok0, own)
                if nres is not None:
                    norm_part2(nres, hTb[(blk + 1) % 2][0], hTb[(blk + 1) % 2][1])

            x_t, bxS = xt[cnt["x"] % NXB]; cnt["x"] += 1
            xS = x_t
            P.op("sp", lambda e: e.dma_start(out=xS[0:NS, :], in_=xs_in), writes=[bxS], dma=bxS)
            s_t, bs = xsb[cnt["xs"] % len(xsb)]; cnt["xs"] += 1
            rms_scale(xS, bxS, NS, gin, bgin, s_t, bs, 3)
            transpose_to(s_t, bs, NS, lambda: hS[:, :, :], bhS, "act")
            hapS = lambda kc: hS[:, kc, :]
            for c in range(4):
                bt, bb = proj_fm(Wi, bWi, c * 128, hapS, bhS, NS)
                P.op("act", (lambda c, bt: lambda e: e.activation(out=kS[:, c, :], in_=bt[:, 0:NS], func=AF.Copy))(c, bt), writes=[bb, bkS])
            st, bst = kvst[cnt["kv"] % 2]; cnt["kv"] += 1
            bt, bb = proj_tm(Wi, bWi, 0, 512, hapS, bhS, NS)
            P.op("act", (lambda st, bt: lambda e: e.activation(out=st[0:NS, 0:512], in_=bt[0:NS, 0:512], func=AF.Copy))(st, bt), writes=[bb, bst])
            bt2, bb2 = proj_tm(Wi, bWi, 512, 512, hapS, bhS, NS)
            P.op("dve", (lambda st, bt2: lambda e: e.tensor_copy(out=st[0:NS, 512:1024], in_=bt2[0:NS, 0:512]))(st, bt2), writes=[bb2, bst])
            P.op("pool", (lambda st: lambda e: e.tensor_copy(out=vSb[:, :], in_=st[0:NS, 512:1024]))(st), reads=[bst], writes=[bvSb])
            P.op("pool", (lambda st: lambda e: e.dma_start(out=swk_d[:, :], in_=st[0:NS, 0:512]))(st), reads=[bst], dma=bst, out=True)
            P.op("pool", (lambda st: lambda e: e.dma_start(out=swv_d[:, :], in_=st[0:NS, 512:1024]))(st), reads=[bst], dma=bst, out=True)
        collect(ph)
        bar = sorted(freed)

        pbn = [0]

        def phase_buf(name, shape, dt, stack):
            pbn[0] += 1
            t, b = sb("%s_p%d" % (name, pbn[0]), shape, dt, stack)
            b.readers = list(bar)
            return t, b

        ph = contextlib.ExitStack()
        with ph:
            WrB, bWconv = phase_buf("WrB", [128, 8, 1536], BF16, ph)
            bWmem = Buf("WrBmem")
            bWmem.readers = list(bar)
            convw, bconvw = phase_buf("convw_t", [128, 2, 3], F32, ph)
            for (d0, s0, n_, bw_) in ((256, 256, 512, bWconv), (1024, 2560, 256, bWconv), (0, 0, 256, bWconv),
                                      (768, 2304, 256, bWmem), (1280, 3328, 256, bWmem)):
                for k0 in (0, 4):
                    P.op("pool", (lambda k0, d0, s0, n_: lambda e: e.dma_start(
                        out=WrB[:, k0:k0 + 4, d0:d0 + n_],
                        in_=w_in[k0 * 128:(k0 + 4) * 128, s0:s0 + n_].rearrange("(k p) n -> p k n", p=128)))(k0, d0, s0, n_),
                         writes=[bw_], dma=bw_)
            P.op("sp", lambda e: e.dma_start(out=convw[:], in_=convw_d), writes=[bconvw], dma=bconvw)
            CB, CC, CH, CQ, CMQ, CZ = 0, 256, 512, 768, 1280, 1536

            def wsel(X):
                if CQ <= X < CMQ:
                    return WrA, bWrA, X - CQ
                if CZ + 256 <= X < CZ + 768:
                    return WrA, bWrA, 512 + X - (CZ + 256)
                if X < CQ:
                    return WrB, bWconv, X
                if CMQ <= X < CZ:
                    return WrB, bWmem, 768 + X - CMQ
                if CZ <= X < CZ + 256:
                    return WrB, bWconv, 1024 + X - CZ
                return WrB, bWmem, 1280 + X - (CZ + 768)
            ubuf, bubuf = phase_buf("ubuf", [128, 2, 2 + 512], F32, ph)
            ccs, bccs = phase_buf("ccs", [128, 512], F32, ph)
            tcv, btcv = phase_buf("tcv", [128, 512], F32, ph)
            szl, bszl = phase_buf("szl", [128, 512], F32, ph)
            mqT, bmqT = phase_buf("mqT", [128, 2, 512], BF16, ph)
            pmT = [phase_buf("pmT%d" % i, [128, 512], BF16, ph) for i in range(2)]
            rdn, brdn = phase_buf("rdn", [128, 512], F32, ph)
            utok, butok = rdn, brdn
            cnt = {"x": 0, "xs": 0, "pm": 0}

            P.op("pool", lambda e: e.memset(ubuf[:], 0.0), writes=[bubuf])

            def normB1(src, r0s):
                res = []
                for tl, r0 in enumerate(r0s):
                    x_t, bx = xt[cnt["x"] % NXB]; cnt["x"] += 1
                    P.op("sp", (lambda x_t, r0: lambda e: e.dma_start(out=x_t[:], in_=src[r0:r0 + 128, :]))(x_t, r0),
                         writes=[bx], dma=bx)
                    s_t, bs = xsb[cnt["xs"] % len(xsb)]; cnt["xs"] += 1
                    rms_scale(x_t, bx, 128, gin, bgin, s_t, bs, 1 + (tl % 2))
                    res.append((s_t, bs))
                return res

            def normB2(res, h_t, bh):
                for tl, (s_t, bs) in enumerate(res):
                    transpose_to(s_t, bs, 128, (lambda tl, h_t: lambda: h_t[:, :, tl * 128:(tl + 1) * 128])(tl, h_t), bh,
                                 "act" if tl % 2 == 0 else "dve")

            def load_norm_tiles(src, r0s, h_t, bh):
                normB2(normB1(src, r0s), h_t, bh)

            def conv_u(hap, bh, n, ub_ap_fn, bub):
                for ch in range(2):
                    btc, bbc = proj_fm(*wsel(CC + ch * 128), hap, bh, n)
                    P.op("act", (lambda btc: lambda e: e.activation(out=ccs[:, 0:n], in_=btc[:, 0:n], func=AF.Copy))(btc),
                         writes=[bbc, bccs])
                    bth, bbh = proj_fm(*wsel(CH + ch * 128), hap, bh, n)
                    P.op("dve", (lambda ch, bth: lambda e: e.tensor_tensor(out=ub_ap_fn(ch), in0=ccs[:, 0:n], in1=bth[:, 0:n],
                                                                           op=ALU.mult))(ch, bth),
                         reads=[bccs], writes=[bbh, bub])

            load_norm_tiles(xo, [tl * 128 for tl in range(4)], hTb[0][0], hTb[0][1])
            for blk in range(4):
                h_t, bh = hTb[blk % 2]
                nres = None
                if blk + 1 < 4 and blk != 0:
                    nres = normB1(xo, [(blk + 1) * 512 + tl * 128 for tl in range(4)])
                hap512 = (lambda h_t: lambda kc: h_t[:, kc, :])(h_t)
                o0 = blk * 512
                for c in range(4):
                    bt, bb = proj_fm(*wsel(CQ + c * 128), hap512, bh, 512)
                    P.op("dve", (lambda c, bt, o0: lambda e: e.tensor_copy(out=qT[:, c, o0:o0 + 512], in_=bt[:, :]))(c, bt, o0),
                         writes=[bb, bqT])
                for c in range(4):
                    bt, bb = proj_fm(*wsel(CZ + (2 + c) * 128), hap512, bh, 512)
                    P.op("act", (lambda c, bt, o0: lambda e: e.activation(out=szA[:, c, o0:o0 + 512], in_=bt[:, :], func=AF.Silu))(c, bt, o0),
                         writes=[bb, bszA])
                if STAGE < 2:
                    continue
                if blk == 0:
                    hh_t, bhh = hTb[1]
                    load_norm_tiles(xh, [TH - 128], hh_t, bhh)
                    conv_u((lambda hh_t: lambda kc: hh_t[:, kc, 0:128])(hh_t), bhh, 128, lambda ch: ubuf[:, ch, 2:130], bubuf)
                    P.op("pool", lambda e: e.tensor_copy(out=ubuf[:, :, 0:2], in_=ubuf[:, :, 128:130]), reads=[bubuf], writes=[bubuf])
                    nres = normB1(xo, [512 + tl * 128 for tl in range(4)])
                conv_u(hap512, bh, 512, lambda ch: ubuf[:, ch, 2:514], bubuf)
                for ch in range(2):
                    bt, bb = proj_fm(*wsel(CZ + ch * 128), hap512, bh, 512)
                    P.op("act", (lambda bt: lambda e: e.activation(out=szl[:, :], in_=bt[:, :], func=AF.Silu))(bt),
                         writes=[bb, bszl])
                    P.op("dve", (lambda ch: lambda e: e.tensor_scalar(out=tcv[:, :], in0=ubuf[:, ch, 0:512],
                                                                      scalar1=convw[:, ch, 0:1], scalar2=None, op0=ALU.mult))(ch),
                         reads=[bubuf, bconvw], writes=[btcv])
                    for j in (1, 2):
                        P.op("dve", (lambda ch, j: lambda e: e.scalar_tensor_tensor(
                            out=tcv[:, :], in0=ubuf[:, ch, j:j + 512], scalar=convw[:, ch, j:j + 1], in1=tcv[:, :],
                            op0=ALU.mult, op1=ALU.add))(ch, j),
                             reads=[bubuf, bconvw, btcv], writes=[btcv])
                    btb, bbb = proj_fm(*wsel(CB + ch * 128), hap512, bh, 512)
                    P.op("dve", (lambda btb: lambda e: e.tensor_tensor(out=tcv[:, :], in0=tcv[:, :], in1=btb[:, :], op=ALU.mult))(btb),
                         reads=[btcv], writes=[bbb, btcv])
                    P.op("dve", (lambda ch, o0: lambda e: e.tensor_tensor(out=mixA[:, ch, o0:o0 + 512], in0=tcv[:, :], in1=szl[:, :], op=ALU.mult))(ch, o0),
                         reads=[btcv, bszl], writes=[bmixA])
                if blk == 3:
                    bt, bb = proj_tm(*wsel(CC), 512, (lambda h_t: lambda kc: h_t[:, kc, 384:512])(h_t), bh)
                    P.op("act", (lambda bt: lambda e: e.activation(out=utok[:, 0:512], in_=bt[:, 0:512], func=AF.Copy))(bt), writes=[bb, butok])
                    P.op("pool", lambda e: e.tensor_tensor(out=utok[:, 0:256], in0=utok[:, 0:256], in1=utok[:, 256:512], op=ALU.mult),
                         reads=[butok], writes=[butok])
                    P.op("pool", lambda e: e.dma_start(out=pconv_d[:, :], in_=utok[126:128, 0:256]), reads=[butok], dma=butok, out=True)
                P.op("pool", lambda e: e.tensor_copy(out=ubuf[:, :, 0:2], in_=ubuf[:, :, 512:514]),
                     reads=[bubuf], writes=[bubuf])
                for cm in range(2):
                    bt, bb = proj_fm(*wsel(CMQ + cm * 128), hap512, bh, 512)
                    P.op("act", (lambda cm, bt: lambda e: e.activation(out=mqT[:, cm, :], in_=bt[:, :], func=AF.Copy))(cm, bt),
                         writes=[bb, bmqT])
                gate = [(szl, bszl), (ccs, bccs)]
                for cm in range(2):
                    btz, bbz = proj_fm(*wsel(CZ + (6 + cm) * 128), hap512, bh, 512)
                    P.op("act", (lambda btz, g_: lambda e: e.activation(out=g_[:, :], in_=btz[:, :], func=AF.Silu))(btz, gate[cm][0]),
                         writes=[bbz, gate[cm][1]])
                for cm in range(2):
                    gz, bgz = gate[cm]
                    for hp in range(2):
                        r0, r1 = hp * 64, hp * 64 + 64
                        ptl = []
                        for mt in range(2):
                            bts, bbs = next_bank()
                            P.op("pe", (lambda bts, mt, r0, r1, cm: lambda e: e.matmul(bts[:, :], lhsT=mkT[r0:r1, cm, mt * 128:(mt + 1) * 128],
                                                                          rhs=mqT[r0:r1, cm, :], start=True, stop=True))(bts, mt, r0, r1, cm),
                                 reads=[bmkT, bmqT], writes=[bbs])
                            pt, bpt = pmT[cnt["pm"] % 2]; cnt["pm"] += 1
                            P.op("act", (lambda pt, bts: lambda e: e.activation(out=pt[:, :], in_=bts[:, :], func=AF.Exp, scale=0.125))(pt, bts),
                                 writes=[bbs, bpt])
                            ptl.append((pt, bpt))
                        bta, bba = next_bank()
                        btd, bbd = next_bank()
                        for mt in range(2):
                            pt, bpt = ptl[mt]
                            P.op("pe", (lambda pt, mt, bta, cm: lambda e: e.matmul(bta[:, :], lhsT=mvb[:, mt, cm * 128:(cm + 1) * 128], rhs=pt[:, :],
                                                                         start=(mt == 0), stop=(mt == 1)))(pt, mt, bta, cm),
                                 reads=[bmvb, bpt], writes=[bba])
                        for mt in range(2):
                            pt, bpt = ptl[mt]
                            P.op("pe", (lambda pt, mt, btd: lambda e: e.matmul(btd[:, :], lhsT=ones_b[:, :], rhs=pt[:, :],
                                                                         start=(mt == 0), stop=(mt == 1)))(pt, mt, btd),
                                 reads=[bones, bpt], writes=[bbd])
                        P.op("act", (lambda btd, r0, r1: lambda e: e.activation(out=rdn[r0:r1, :], in_=btd[r0:r1, :], func=AF.Ln))(btd, r0, r1),
                             writes=[bbd, brdn])
                        P.op("act", (lambda r0, r1: lambda e: e.activation(out=rdn[r0:r1, :], in_=rdn[r0:r1, :], func=AF.Exp, scale=-1.0))(r0, r1),
                             writes=[brdn])
                        P.op("dve", (lambda bta, r0, r1: lambda e: e.tensor_tensor(out=rdn[r0:r1, :], in0=rdn[r0:r1, :], in1=bta[r0:r1, :], op=ALU.mult))(bta, r0, r1),
                             reads=[brdn], writes=[bba, brdn])
                        P.op("dve", (lambda r0, r1, cm, o0, gz: lambda e: e.tensor_tensor(out=mixA[r0:r1, 2 + cm, o0:o0 + 512], in0=rdn[r0:r1, :], in1=gz[r0:r1, :], op=ALU.mult))(r0, r1, cm, o0, gz),
                             reads=[brdn, bgz], writes=[bmixA])
                if nres is not None:
                    normB2(nres, hTb[(blk + 1) % 2][0], hTb[(blk + 1) % 2][1])

            for c in range(4):
                bt, bb = proj_fm(*wsel(CQ + c * 128), hapS, bhS, NS)
                P.op("act", (lambda c, bt: lambda e: e.activation(out=qS[:, c, :], in_=bt[:, 0:NS], func=AF.Copy))(c, bt), writes=[bb, bqS])
            for c in range(2):
                bt, bb = proj_fm(*wsel(CMQ + c * 128), hapS, bhS, NS)
                P.op("act", (lambda c, bt: lambda e: e.activation(out=mqS[:, c, :], in_=bt[:, 0:NS], func=AF.Copy))(c, bt), writes=[bb, bmqS])
                bt, bb = proj_fm(*wsel(CB + c * 128), hapS, bhS, NS)
                P.op("act", (lambda c, bt: lambda e: e.activation(out=cbS[:, c, :], in_=bt[:, 0:NS], func=AF.Copy))(c, bt), writes=[bb, bcbS])
            for c in range(8):
                bt, bb = proj_fm(*wsel(CZ + c * 128), hapS, bhS, NS)
                P.op("act", (lambda c, bt: lambda e: e.activation(out=szS[:, c, :], in_=bt[:, 0:NS], func=AF.Silu))(c, bt), writes=[bb, bszS])
            bt, bb = proj_tm(*wsel(CC), 512, hapS, bhS, NS)
            P.op("act", (lambda bt: lambda e: e.activation(out=utok[0:NS, 0:512], in_=bt[0:NS, 0:512], func=AF.Copy))(bt), writes=[bb, butok])
            P.op("pool", lambda e: e.tensor_tensor(out=utok[0:NS, 0:256], in0=utok[0:NS, 0:256], in1=utok[0:NS, 256:512], op=ALU.mult),
                 reads=[butok], writes=[butok])
            for b in range(NB):
                P.op("pool", (lambda b: lambda e: e.dma_start(out=sconv_d[b], in_=utok[b * 4 + 2:b * 4 + 4, 0:256]))(b),
                     reads=[butok], dma=butok, out=True)
            if STAGE >= 3:
                cct, bcct = phase_buf("cct", [32, 256], F32, ph)
                P.op("sp", lambda e: e.dma_start(out=cct[:], in_=ccv), writes=[bcct], dma=bcct)
                btt, bbt = next_bank()
                for ch in range(2):
                    P.op("pe", (lambda ch: lambda e: e.transpose(btt[:, ch * 32:(ch + 1) * 32], cct[0:32, ch * 128:(ch + 1) * 128],
                                                                 ident_f[0:32, 0:32]))(ch),
                         reads=[bcct, bidf], writes=[bbt])
                P.op("dve", lambda e: e.tensor_copy(out=uS[:, :, :, 0:2],
                                                    in_=btt[:, 0:64].rearrange("p (c b r) -> p c b r", c=2, r=2)),
                     writes=[bbt, buS])
                for ch in range(2):
                    btc, bbc = proj_fm(*wsel(CC + ch * 128), hapS, bhS, NS)
                    P.op("act", (lambda btc: lambda e: e.activation(out=ccs[:, 0:NS], in_=btc[:, 0:NS], func=AF.Copy))(btc), writes=[bbc, bccs])
                    bth, bbh = proj_fm(*wsel(CH + ch * 128), hapS, bhS, NS)
                    P.op("dve", (lambda ch, bth: lambda e: e.tensor_tensor(
                        out=uS[:, ch, :, 2:6], in0=ccs[:, 0:NS].rearrange("p (b j) -> p b j", j=4),
                        in1=bth[:, 0:NS].rearrange("p (b j) -> p b j", j=4), op=ALU.mult))(ch, bth),
                         reads=[bccs], writes=[bbh, buS])
                    tv = tcv[:, 0:NS].rearrange("p (b j) -> p b j", j=4)
                    P.op("dve", (lambda ch, tv: lambda e: e.tensor_scalar(out=tv, in0=uS[:, ch, :, 0:4], scalar1=convw[:, ch, 0:1],
                                                                      scalar2=None, op0=ALU.mult))(ch, tv),
                         reads=[buS, bconvw], writes=[btcv])
                    for j in (1, 2):
                        P.op("dve", (lambda ch, j, tv: lambda e: e.scalar_tensor_tensor(
                            out=tv, in0=uS[:, ch, :, j:j + 4], scalar=convw[:, ch, j:j + 1], in1=tv, op0=ALU.mult, op1=ALU.add))(ch, j, tv),
                             reads=[buS, bconvw, btcv], writes=[btcv])
                    P.op("dve", (lambda ch: lambda e: e.tensor_tensor(out=tcv[:, 0:NS], in0=tcv[:, 0:NS], in1=cbS[:, ch, :], op=ALU.mult))(ch),
                         reads=[btcv, bcbS], writes=[btcv])
                    P.op("dve", (lambda ch: lambda e: e.tensor_tensor(out=mixS[:, ch, :], in0=tcv[:, 0:NS], in1=szS[:, ch, :], op=ALU.mult))(ch),
                         reads=[btcv, bszS], writes=[bmixS])

        pAB.close()
        pr.close()
        collect(ph, [bWmem, bWrA])
        collect(pAB)
        bar = sorted(freed)

        prS = contextlib.ExitStack()
        es.enter_context(prS)

        def right_buf(name, shape, dt):
            t = prS.enter_context(nc.sbuf_tensor("s_" + name, list(shape), dt, side="right"))
            b = Buf(name)
            b.readers = list(bar)
            return t, b

        Qbd, bQbd = right_buf("Qbd", [128, NB, 4, 32], BF16)
        Qbdm, bQbdm = right_buf("Qbdm", [128, NB, 2, 16], BF16)
        Kt0 = right_buf("Kt0r", [128, 8, 512], BF16)
        Vt0 = right_buf("Vt0r", [128, 8, 512], BF16)
        mkb0 = right_buf("mkb0r", [128, 2, 256], BF16)
        mvs0 = right_buf("mvs0r", [128, 2, 256], BF16)
        def prefetch_batch0():
            P.op("pool", lambda e: e.memset(Qbd[:], 0.0), writes=[bQbd])
            P.op("pool", lambda e: e.memset(Qbdm[:], 0.0), writes=[bQbdm])
            for c in range(4):
                for half in range(2):
                    h = 2 * c + half
                    r0, r1 = half * 64, half * 64 + 64
                    P.op("dve", (lambda c, h, r0, r1: lambda e: e.tensor_copy(
                        out=Qbd[r0:r1, :, c, h * 4:h * 4 + 4], in_=qS[r0:r1, c, :].rearrange("p (b j) -> p b j", j=4)))(c, h, r0, r1),
                         reads=[bqS], writes=[bQbd])
            for cm in range(2):
                for half in range(2):
                    h = 2 * cm + half
                    r0, r1 = half * 64, half * 64 + 64
                    P.op("dve", (lambda cm, h, r0, r1: lambda e: e.tensor_copy(
                        out=Qbdm[r0:r1, :, cm, h * 4:h * 4 + 4], in_=mqS[r0:r1, cm, :].rearrange("p (b j) -> p b j", j=4)))(cm, h, r0, r1),
                         reads=[bmqS], writes=[bQbdm])

            for (tt_, bb__) in (Kt0, Vt0):
                P.op("pool", (lambda tt_: lambda e: e.memset(tt_[64:128, 0:4, :], 0.0))(tt_), writes=[bb__])
            for (dstb, srcd) in ((Kt0, cwk), (Vt0, cwv)):
                s16 = bass.AP(tensor=srcd.tensor, offset=0, ap=[[16 * 512, 96], [1, 2048]])
                s4 = bass.AP(tensor=srcd.tensor, offset=1536 * 512, ap=[[2048, 128], [1, 2048]])
                P.op("pool", (lambda dst, s16: lambda e: e.dma_start(out=dst[0:96, 0:4, :].rearrange("p a n -> p (a n)"), in_=s16))(dstb[0], s16),
                     writes=[dstb[1]], dma=dstb[1])
                P.op("pool", (lambda dst, s4: lambda e: e.dma_start(out=dst[:, 4:8, :].rearrange("p a n -> p (a n)"), in_=s4))(dstb[0], s4),
                     writes=[dstb[1]], dma=dstb[1])
            P.op("pool", lambda e: e.dma_start(out=mkb0[0][:], in_=cmk[0].rearrange("(t p) n -> p t n", p=128)), writes=[mkb0[1]], dma=mkb0[1])
            P.op("pool", lambda e: e.dma_start(out=mvs0[0][:], in_=cmv[0].rearrange("(t p) n -> p t n", p=128)), writes=[mvs0[1]], dma=mvs0[1])


        if STAGE >= 2:
            pc = contextlib.ExitStack()
            with pc:
                accs = [phase_buf("acc%d" % i, [128, 2, T], F32, pc) for i in range(2)]
                btabs = [phase_buf("btab%d" % i, [128, 3 * 2 * 512], BF16, pc) for i in range(2)]
                vds = [phase_buf("vd%d" % i, [128, 32, 128], BF16, pc) for i in range(2)]
                pts = [phase_buf("ptT%d" % i, [128, 512], BF16, pc) for i in range(4)]

                items = []
                groups = []
                for pair in range(4):
                    for ci, dil in enumerate(DILS):
                        g = len(groups)
                        groups.append((pair, ci, dil))
                        nq = 16 // dil
                        qtiles = [(r, qt) for r in range(dil) for qt in range(nq)]
                        for u0 in range(0, 16, 2):
                            for hp in range(2):
                                items.append(dict(pair=pair, ci=ci, dil=dil, g=g, tl=qtiles[u0:u0 + 2], hp=hp,
                                                  first=(u0 == 0 and hp == 0), last_of_pair=(ci == 2 and u0 == 14 and hp == 1)))

                def load_group(g):
                    pair, ci, dil = groups[g]
                    vd, bvd = vds[g % 2]
                    ntile = 32 // dil
                    for r in range(dil):
                        j0 = 15 if dil == 1 else 0
                        src = bass.AP(tensor=vs_d.tensor, offset=(pair * (TH + T) + r + j0 * 128 * dil) * 128,
                                      ap=[[dil * 128, 128], [128 * dil * 128, ntile - j0], [1, 128]])
                        P.op("sp", (lambda vd, r, ntile, j0, src: lambda e: e.dma_start(
                            out=vd[:, r * ntile + j0:(r + 1) * ntile, :], in_=src))(vd, r, ntile, j0, src),
                             reads=[bvs], writes=[bvd], dma=bvd)

                def load_btab(pair):
                    bt_t, bbt_ = btabs[pair % 2]
                    P.op("pool", (lambda bt_t, pair: lambda e: e.dma_start(out=bt_t[:], in_=btab_d[pair]))(bt_t, pair),
                         writes=[bbt_], dma=bbt_)

                def front_mms(k):
                    it = items[k]
                    pair, ci, dil, hp, tl = it["pair"], it["ci"], it["dil"], it["hp"], it["tl"]
                    r0, r1 = hp * 64, hp * 64 + 64
                    starts = [qt * 128 * dil + r for (r, qt) in tl]
                    bX, bbX = next_bank()
                    bt_t, bbt_ = btabs[pair % 2]
                    tb0 = (ci * 2 + hp) * 512
                    mms = [(lambda bX, bbX, bt_t, bbt_, tb0: lambda: P.op("pe", lambda e: e.matmul(
                        bX[:, :], lhsT=ident_b[:, :], rhs=bt_t[:, tb0:tb0 + 512], start=True, stop=False),
                        reads=[bidb, bbt_], writes=[bbX]))(bX, bbX, bt_t, bbt_, tb0)]
                    nmm = 0
                    for ui, (r, qt) in enumerate(tl):
                        qs = starts[ui]
                        q_ap = qT[r0:r1, pair, qs:qs + 127 * dil + 1:dil]
                        for ab in range(2):
                            ks = TH + qs - (1 - ab) * 128 * dil
                            k_ap = kT[r0:r1, pair, ks:ks + 127 * dil + 1:dil]
                            col = ui * 256 + ab * 128
                            nmm += 1
                            mms.append((lambda bX, bbX, col, k_ap, q_ap, last: lambda: P.op("pe", lambda e: e.matmul(
                                bX[:, col:col + 128], lhsT=k_ap, rhs=q_ap, start=False, stop=last),
                                reads=[bkT, bqT], writes=[bbX]))(bX, bbX, col, k_ap, q_ap, nmm == 4))
                    return mms, (bX, bbX)

                def front_ew(k, ctx):
                    bX, bbX = ctx
                    ptt, bpt = pts[k % len(pts)]
                    P.op("act", (lambda bX, ptt: lambda e: e.activation(out=ptt[:, :], in_=bX[:, :], func=AF.Exp, scale=0.125))(bX, ptt),
                         writes=[bbX, bpt])

                def front2(k):
                    m0, c0_ = front_mms(k)
                    m1, c1_ = front_mms(k + 1)
                    for f0, f1 in zip(m0, m1):
                        f0()
                        f1()
                    front_ew(k, c0_)
                    front_ew(k + 1, c1_)

                def back(k):
                    it = items[k]
                    pair, ci, dil, hp, tl, g = it["pair"], it["ci"], it["dil"], it["hp"], it["tl"], it["g"]
                    r0, r1 = hp * 64, hp * 64 + 64
                    ntile = 32 // dil
                    vd, bvd = vds[g % 2]
                    ptt, bpt = pts[k % len(pts)]
                    acc, bacc = accs[pair % 2]
                    starts = [qt * 128 * dil + r for (r, qt) in tl]
                    delta = starts[1] - starts[0]
                    bY, bbY = next_bank()
                    for ui, (r, qt) in enumerate(tl):
                        for ab in range(2):
                            J = 16 // dil + qt - (1 - ab)
                            col = ui * 256 + ab * 128
                            P.op("pe", (lambda bY, ui, ab, vd, tix, ptt, col: lambda e: e.matmul(
                                bY[:, ui * 128:(ui + 1) * 128], lhsT=vd[:, tix, :], rhs=ptt[:, col:col + 128],
                                start=(ab == 0), stop=(ab == 1)))(bY, ui, ab, vd, r * ntile + J, ptt, col),
                                 reads=[bvd, bpt], writes=[bbY])
                    haloA = [(16 // dil + qt - 1) < 16 // dil for (r, qt) in tl]
                    if haloA[0] == haloA[1]:
                        pv4 = ptt[:, :].rearrange("p (u ab m) -> p u ab m", u=2, ab=2)
                        for ab in range(2):
                            on = hones_b if (ab == 0 and haloA[0]) else ones_b
                            P.op("pe", (lambda bY, ab, on, pv4: lambda e: e.matmul(
                                bY[:, 256:512], lhsT=on[:, :], rhs=pv4[:, :, ab, :], start=(ab == 0), stop=(ab == 1)))(bY, ab, on, pv4),
                                 reads=[bhob, bones, bpt], writes=[bbY])
                    else:
                        for ui, (r, qt) in enumerate(tl):
                            for ab in range(2):
                                J = 16 // dil + qt - (1 - ab)
                                halo = J < 16 // dil
                                col = ui * 256 + ab * 128
                                on = hones_b if halo else ones_b
                                P.op("pe", (lambda bY, ui, ab, on, ptt, col: lambda e: e.matmul(
                                    bY[:, 256 + ui * 128:256 + (ui + 1) * 128], lhsT=on[:, :], rhs=ptt[:, col:col + 128],
                                    start=(ab == 0), stop=(ab == 1)))(bY, ui, ab, on, ptt, col),
                                     reads=[bhob, bones, bpt], writes=[bbY])
                    dst = sub_ap(acc[r0:r1, 0, starts[0]:starts[0] + 1], [[T, 2], [delta, 2], [dil, 128]])
                    srcp = bY[r0:r1, :].rearrange("p (w u m) -> p w u m", w=2, m=128)
                    if ci == 0 and hp == 0:
                        P.op("act", (lambda dst, srcp: lambda e: e.activation(out=dst, in_=srcp, func=AF.Copy))(dst, srcp),
                             writes=[bbY, bacc])
                    elif ci == 0:
                        P.op("dve", (lambda dst, srcp: lambda e: e.tensor_copy(out=dst, in_=srcp))(dst, srcp),
                             writes=[bbY, bacc])
                    else:
                        P.op("dve", (lambda dst, srcp: lambda e: e.tensor_tensor(out=dst, in0=srcp, in1=dst, op=ALU.add))(dst, srcp),
                             writes=[bbY, bacc])
                    if it["last_of_pair"]:
                        for q4 in range(4):
                            cs = slice(q4 * 512, (q4 + 1) * 512)
                            pending.append((lambda acc, bacc, cs, pair: lambda: (
                                P.op("act", lambda e: e.activation(out=acc[:, 1, cs], in_=acc[:, 1, cs], func=AF.Ln), writes=[bacc]),
                                P.op("act", lambda e: e.activation(out=acc[:, 1, cs], in_=acc[:, 1, cs], func=AF.Exp, scale=-1.0), writes=[bacc]),
                                P.op("dve" if flush_on_dve[0] else "pool",
                                     lambda e: e.tensor_tensor(out=acc[:, 0, cs], in0=acc[:, 0, cs], in1=acc[:, 1, cs], op=ALU.mult),
                                     writes=[bacc]),
                                P.op("dve" if flush_on_dve[0] else "pool",
                                     lambda e: e.tensor_tensor(out=mixB[:, pair, cs], in0=acc[:, 0, cs], in1=szA[:, pair, cs], op=ALU.mult),
                                     reads=[bacc, bszA], writes=[bmixB])))(acc, bacc, cs, pair))

                NI = len(items)
                assert NI % 2 == 0
                pending = []
                flush_on_dve = [False]
                load_btab(0)
                load_btab(1)
                load_group(0)
                load_group(1)
                if STAGE >= 3:
                    prefetch_batch0()
                front2(0)
                for k in range(0, NI, 2):
                    it = items[k]
                    if k + 2 < NI:
                        nx = items[k + 2]
                        front2(k + 2)
                        if nx["first"] and nx["ci"] == 0 and 0 < nx["pair"] < 3:
                            load_btab(nx["pair"] + 1)
                    back(k)
                    if pending and not items[k + 1]["last_of_pair"]:
                        pending.pop(0)()
                    back(k + 1)
                    if k + 2 == NI or items[k + 2]["g"] != it["g"]:
                        if it["g"] + 2 < len(groups):
                            load_group(it["g"] + 2)
                flush_on_dve[0] = True
                while pending:
                    pending.pop(0)()

            collect(pc)

        pk.close()
        collect(pk)
        bar = sorted(freed)
        if STAGE >= 3:
            psx = contextlib.ExitStack()
            with psx:
                wmain, bwmain = phase_buf("wmain", [32, 1024], F32, psx)
                wnew, bwnew = phase_buf("wnew", [32, NB * 64], F32, psx)
                bmask, bbmask = phase_buf("bmask", [32, 512], F32, psx)
                bmaskm, bbmaskm = phase_buf("bmaskm", [16, 256], F32, psx)
                par, bpar = phase_buf("par", [32, 2], F32, psx)
                for tt, bb_, dd in ((wmain, bwmain, wmain_d), (wnew, bwnew, wnew_d), (bmask, bbmask, bmask_d),
                                    (bmaskm, bbmaskm, bmaskm_d), (par, bpar, par_d)):
                    P.op("sp", (lambda tt, dd: lambda e: e.dma_start(out=tt[:], in_=dd))(tt, dd), writes=[bb_], dma=bb_)
                Oall, bOall = phase_buf("Oall", [32, NB, 64], F32, psx)
                Omall, bOmall = phase_buf("Omall", [16, NB, 64], F32, psx)
                Opad, bOpad = phase_buf("Opad", [32, NB, 128], F32, psx)
                Ompad, bOmpad = phase_buf("Ompad", [16, NB, 128], F32, psx)
                L1, bL1 = phase_buf("L1", [32, NB], F32, psx)
                L2, bL2 = phase_buf("L2", [32, NB], F32, psx)
                Lm, bLm = phase_buf("Lm", [16, NB], F32, psx)
                Kt = [Kt0, phase_buf("Kt1", [128, 8, 512], BF16, psx)]
                Vt = [Vt0, phase_buf("Vt1", [128, 8, 512], BF16, psx), phase_buf("Vt2", [128, 8, 512], BF16, psx)]
                KTs = [phase_buf("KTs%d" % i, [128, 4, 1024], BF16, psx) for i in range(2)]
                mkb = [mkb0, phase_buf("mkb1", [128, 2, 256], BF16, psx)]
                mvs = [mvs0, phase_buf("mvs1", [128, 2, 256], BF16, psx), phase_buf("mvs2", [128, 2, 256], BF16, psx)]
                mkTs = [phase_buf("mkTs%d" % i, [128, 2, 256], BF16, psx) for i in range(2)]
                Eb = [phase_buf("Eb%d" % i, [32, 1088], F32, psx) for i in range(2)]
                Pb = [phase_buf("Pb%d" % i, [32, 1088], BF16, psx) for i in range(2)]
                Pmb = [phase_buf("Pmb%d" % i, [16, 256], BF16, psx) for i in range(2)]
                PTb = [phase_buf("PTb%d" % i, [128, 320], BF16, psx) for i in range(2)]
                tmpb = [phase_buf("tmpb%d" % i, [32, 512], F32, psx) for i in range(2)]
                tmpm = [phase_buf("tmpm%d" % i, [16, 256], F32, psx) for i in range(2)]

                def load_k(b):
                    kt, bkt = Kt[b % 2]
                    mk_, bmk_ = mkb[b % 2]
                    s16 = bass.AP(tensor=cwk.tensor, offset=b * 2048 * 512, ap=[[16 * 512, 96], [1, 2048]])
                    s4 = bass.AP(tensor=cwk.tensor, offset=(b * 2048 + 1536) * 512, ap=[[2048, 128], [1, 2048]])
                    P.op("pool", (lambda dst, s16: lambda e: e.dma_start(out=dst[0:96, 0:4, :].rearrange("p a n -> p (a n)"), in_=s16))(kt, s16),
                         writes=[bkt], dma=bkt)
                    P.op("pool", (lambda dst, s4: lambda e: e.dma_start(out=dst[:, 4:8, :].rearrange("p a n -> p (a n)"), in_=s4))(kt, s4),
                         writes=[bkt], dma=bkt)
                    P.op("pool", (lambda mk_, b: lambda e: e.dma_start(out=mk_[:], in_=cmk[b].rearrange("(t p) n -> p t n", p=128)))(mk_, b),
                         writes=[bmk_], dma=bmk_)

                def load_v(b):
                    vt, bvt = Vt[b % 3]
                    mv_, bmv_ = mvs[b % 3]
                    s16 = bass.AP(tensor=cwv.tensor, offset=b * 2048 * 512, ap=[[16 * 512, 96], [1, 2048]])
                    s4 = bass.AP(tensor=cwv.tensor, offset=(b * 2048 + 1536) * 512, ap=[[2048, 128], [1, 2048]])
                    P.op("pool", (lambda dst, s16: lambda e: e.dma_start(out=dst[0:96, 0:4, :].rearrange("p a n -> p (a n)"), in_=s16))(vt, s16),
                         writes=[bvt], dma=bvt)
                    P.op("pool", (lambda dst, s4: lambda e: e.dma_start(out=dst[:, 4:8, :].rearrange("p a n -> p (a n)"), in_=s4))(vt, s4),
                         writes=[bvt], dma=bvt)
                    P.op("pool", (lambda mv_, b: lambda e: e.dma_start(out=mv_[:], in_=cmv[b].rearrange("(t p) n -> p t n", p=128)))(mv_, b),
                         writes=[bmv_], dma=bmv_)

                xr = [phase_buf("xr%d" % i, [128, D], F32, psx) for i in range(2)]
                yo = [phase_buf("yo%d" % i, [128, D], F32, psx) for i in range(2)]
                stat2, bstat2 = phase_buf("stat2", [128, 8], F32, psx)
                Wo, bWo = phase_buf("Wo", [128, 8, D], BF16, psx)
                gfin, bgfin = phase_buf("gfin", [128, D], F32, psx)

                def mix_prompt(c, tsl):
                    if c < 2:
                        return mixA[:, c, tsl]
                    if c < 6:
                        return mixB[:, c - 2, tsl]
                    return mixA[:, c - 4, tsl]

                def out_tile(src_rows, dst_rows, mix_fn, bmx, tsl, n, k):
                    x_t, bx = xr[k % 2]
                    r_t, br = x_t, bx
                    y_t, by = yo[k % 2]
                    P.op("sp", lambda e: e.dma_start(out=x_t[0:n, :], in_=src_rows), writes=[bx], dma=bx)
                    yield
                    hb = []
                    for half in range(2):
                        bt, bb = next_bank("aux")
                        for c in range(8):
                            P.op("pe", (lambda c, bt, half: lambda e: e.matmul(bt[0:n, :], lhsT=mix_fn(c, tsl), rhs=Wo[:, c, half * 512:(half + 1) * 512],
                                                                             start=(c == 0), stop=(c == 7)))(c, bt, half),
                                 reads=list(bmx) + [bWo], writes=[bb])
                        P.op("dve", (lambda bt, half: lambda e: e.tensor_tensor(out=r_t[0:n, half * 512:(half + 1) * 512], in0=bt[0:n, :],
                                                                               in1=x_t[0:n, half * 512:(half + 1) * 512], op=ALU.add))(bt, half),
                             reads=[bx], writes=[bb, br])
                        yield
                    col = k % 8
                    P.op("act", lambda e: e.activation(out=y_t[0:n, :], in_=r_t[0:n, :], func=AF.Square, accum_out=stat2[0:n, col:col + 1]),
                         reads=[br], writes=[bstat2, by])
                    P.op("act", lambda e: e.activation(out=stat2[0:n, col:col + 1], in_=stat2[0:n, col:col + 1], func=AF.Ln,
                                                       scale=1.0 / D, bias=epsb[0:n, 0:1]), reads=[bstat2, bepsb], writes=[bstat2])
                    P.op("act", lambda e: e.activation(out=stat2[0:n, col:col + 1], in_=stat2[0:n, col:col + 1], func=AF.Exp, scale=-0.5),
                         reads=[bstat2], writes=[bstat2])
                    yield
                    P.op("dve", lambda e: e.scalar_tensor_tensor(out=y_t[0:n, :], in0=r_t[0:n, :], scalar=stat2[0:n, col:col + 1], in1=gfin[0:n, :],
                                                                 op0=ALU.mult, op1=ALU.mult), reads=[br, bstat2, bgfin], writes=[by])
                    yield
                    P.op("sp", lambda e: e.dma_start(out=dst_rows, in_=y_t[0:n, :]), reads=[by], dma=by, out=True)
                    yield


                bank_pool["main"] = list(range(6))

                def stageA(b):
                    kt, bkt = Kt[b % 2]; vt, bvt = Vt[b % 3]
                    mk_, bmk_ = mkb[b % 2]; mv_, bmv_ = mvs[b % 3]
                    KT_, bKT_ = KTs[b % 2]; mkT_, bmkT_ = mkTs[b % 2]
                    E_, bE_ = Eb[b % 2]; P_, bP_ = Pb[b % 2]; Pm_, bPm_ = Pmb[b % 2]; PT_, bPT_ = PTb[b % 2]
                    tp_, btp_ = tmpb[b % 2]; tm_, btm_ = tmpm[b % 2]
                    for c in range(4):
                        bt, bb = next_bank()
                        v = bf(bt)
                        for t8 in range(8):
                            P.op("pe", (lambda v, t8, kt, c: lambda e: e.transpose(v[:, t8 * 128:(t8 + 1) * 128], kt[:, t8, c * 128:(c + 1) * 128],
                                                                                 ident_b[:, :]))(v, t8, kt, c),
                                 reads=[bkt, bidb], writes=[bb])
                        if c % 2 == 0:
                            P.op("act", (lambda v, KT_, c: lambda e: e.activation(out=KT_[:, c, :], in_=v[:, :], func=AF.Copy))(v, KT_, c),
                                 writes=[bb, bKT_])
                        else:
                            P.op("dve", (lambda v, KT_, c: lambda e: e.tensor_copy(out=KT_[:, c, :], in_=v[:, :]))(v, KT_, c),
                                 writes=[bb, bKT_])
                        yield
                    bt, bb = next_bank()
                    v = bf(bt)
                    for cm in range(2):
                        for mt in range(2):
                            P.op("pe", (lambda v, cm, mt, mk_: lambda e: e.transpose(v[:, (cm * 2 + mt) * 128:(cm * 2 + mt + 1) * 128],
                                                                                  mk_[:, mt, cm * 128:(cm + 1) * 128], ident_b[:, :]))(v, cm, mt, mk_),
                                 reads=[bmk_, bidb], writes=[bb])
                    P.op("act", (lambda v, mkT_: lambda e: e.activation(out=mkT_[:, :, :].rearrange("p c m -> p (c m)"), in_=v[:, 0:512], func=AF.Copy))(v, mkT_),
                         writes=[bb, bmkT_])
                    yield
                    bS0, bbS0 = next_bank(); bS1, bbS1 = next_bank(); bS2, bbS2 = next_bank(); bSm, bbSm = next_bank()
                    for c in range(4):
                        q_ap = Qbd[:, b, c, :]
                        P.op("pe", (lambda bS0, q_ap, KT_, c: lambda e: e.matmul(bS0[0:32, :], lhsT=q_ap, rhs=KT_[:, c, 0:512], start=(c == 0), stop=(c == 3)))(bS0, q_ap, KT_, c),
                             reads=[bQbd, bKT_], writes=[bbS0])
                        P.op("pe", (lambda bS1, q_ap, KT_, c: lambda e: e.matmul(bS1[0:32, :], lhsT=q_ap, rhs=KT_[:, c, 512:1024], start=(c == 0), stop=(c == 3)))(bS1, q_ap, KT_, c),
                             reads=[bQbd, bKT_], writes=[bbS1])
                        P.op("pe", (lambda bS2, q_ap, c: lambda e: e.matmul(bS2[0:32, 0:NS], lhsT=q_ap, rhs=kS[:, c, :], start=(c == 0), stop=(c == 3)))(bS2, q_ap, c),
                             reads=[bQbd, bkS], writes=[bbS2])
                    for cm in range(2):
                        P.op("pe", (lambda bSm, cm, mkT_, b: lambda e: e.matmul(bSm[0:16, 0:256], lhsT=Qbdm[:, b, cm, :], rhs=mkT_[:, cm, :], start=(cm == 0), stop=(cm == 1)))(bSm, cm, mkT_, b),
                             reads=[bQbdm, bmkT_], writes=[bbSm])
                    yield
                    P.op("act", (lambda E_, bS0: lambda e: e.activation(out=E_[:, 0:512], in_=bS0[0:32, :], func=AF.Exp, scale=0.125))(E_, bS0), writes=[bbS0, bE_])
                    P.op("act", (lambda E_, bS1: lambda e: e.activation(out=E_[:, 512:1024], in_=bS1[0:32, :], func=AF.Exp, scale=0.125))(E_, bS1), writes=[bbS1, bE_])
                    P.op("act", (lambda E_, bS2: lambda e: e.activation(out=E_[:, 1024:1088], in_=bS2[0:32, 0:NS], func=AF.Exp, scale=0.125))(E_, bS2), writes=[bbS2, bE_])
                    P.op("act", (lambda Pm_, bSm, b: lambda e: e.activation(out=Pm_[:, :], in_=bSm[0:16, 0:256], func=AF.Exp, scale=0.125, accum_out=Lm[:, b:b + 1]))(Pm_, bSm, b),
                         writes=[bbSm, bPm_, bLm])
                    yield
                    P.op("dve", (lambda P_, E_, b: lambda e: e.scalar_tensor_tensor(out=P_[:, 0:1024], in0=E_[:, 0:1024], scalar=1.0, in1=wmain[:, :],
                                                                                op0=ALU.mult, op1=ALU.mult, accum_out=L1[:, b:b + 1]))(P_, E_, b),
                         reads=[bE_, bwmain], writes=[bP_, bL1])
                    P.op("dve", (lambda P_, E_, b: lambda e: e.scalar_tensor_tensor(out=P_[:, 1024:1088], in0=E_[:, 1024:1088], scalar=1.0, in1=wnew[:, b * 64:(b + 1) * 64],
                                                                                op0=ALU.mult, op1=ALU.mult, accum_out=L2[:, b:b + 1]))(P_, E_, b),
                         reads=[bE_, bwnew], writes=[bP_, bL2])

                def stageB(b):
                    kt, bkt = Kt[b % 2]; vt, bvt = Vt[b % 3]
                    mk_, bmk_ = mkb[b % 2]; mv_, bmv_ = mvs[b % 3]
                    KT_, bKT_ = KTs[b % 2]; mkT_, bmkT_ = mkTs[b % 2]
                    E_, bE_ = Eb[b % 2]; P_, bP_ = Pb[b % 2]; Pm_, bPm_ = Pmb[b % 2]; PT_, bPT_ = PTb[b % 2]
                    tp_, btp_ = tmpb[b % 2]; tm_, btm_ = tmpm[b % 2]
                    bt, bb = next_bank()
                    v = bf(bt)
                    for t8 in range(8):
                        P.op("pe", (lambda v, t8, P_: lambda e: e.transpose(v[:, t8 * 32:(t8 + 1) * 32], P_[0:32, t8 * 128:(t8 + 1) * 128], ident_b[0:32, 0:32]))(v, t8, P_),
                             reads=[bP_, bidb], writes=[bb])
                    P.op("pe", (lambda v, P_: lambda e: e.transpose(v[0:64, 256:288], P_[0:32, 1024:1088], ident_b[0:32, 0:32]))(v, P_),
                         reads=[bP_, bidb], writes=[bb])
                    for mt in range(2):
                        P.op("pe", (lambda v, mt, Pm_: lambda e: e.transpose(v[:, 288 + mt * 16:288 + (mt + 1) * 16], Pm_[0:16, mt * 128:(mt + 1) * 128], ident_b[0:16, 0:16]))(v, mt, Pm_),
                             reads=[bPm_, bidb], writes=[bb])
                    yield
                    P.op("dve", (lambda v, PT_: lambda e: e.tensor_copy(out=PT_[:, :], in_=v[:, 0:320]))(v, PT_), writes=[bb, bPT_])
                    yield
                    bO, bbO = next_bank(); bOm, bbOm = next_bank()
                    for t8 in range(8):
                        P.op("pe", (lambda bO, t8, PT_, vt: lambda e: e.matmul(bO[0:32, :], lhsT=PT_[:, t8 * 32:(t8 + 1) * 32], rhs=vt[:, t8, :], start=(t8 == 0), stop=False))(bO, t8, PT_, vt),
                             reads=[bPT_, bvt], writes=[bbO])
                    P.op("pe", (lambda bO, PT_: lambda e: e.matmul(bO[0:32, :], lhsT=PT_[0:64, 256:288], rhs=vSb[0:64, :], start=False, stop=True))(bO, PT_),
                         reads=[bPT_, bvSb], writes=[bbO])
                    for mt in range(2):
                        P.op("pe", (lambda bOm, mt, PT_, mv_: lambda e: e.matmul(bOm[0:16, 0:256], lhsT=PT_[:, 288 + mt * 16:288 + (mt + 1) * 16], rhs=mv_[:, mt, :], start=(mt == 0), stop=(mt == 1)))(bOm, mt, PT_, mv_),
                             reads=[bPT_, bmv_], writes=[bbOm])
                    yield
                    P.op("dve", (lambda tp_, bO: lambda e: e.tensor_tensor(out=tp_[:, :], in0=bO[0:32, :], in1=bmask[:, :], op=ALU.mult))(tp_, bO),
                         reads=[bbmask], writes=[bbO, btp_])
                    P.op("dve", (lambda tp_, b: lambda e: e.tensor_reduce(out=Oall[:, b, :], in_=tp_[:, :].rearrange("p (h d) -> p d h", d=64), axis=AX.X, op=ALU.add))(tp_, b),
                         reads=[btp_], writes=[bOall])
                    yield
                    P.op("dve", (lambda tm_, bOm: lambda e: e.tensor_tensor(out=tm_[:, :], in0=bOm[0:16, 0:256], in1=bmaskm[:, :], op=ALU.mult))(tm_, bOm),
                         reads=[bbmaskm], writes=[bbOm, btm_])
                    P.op("dve", (lambda tm_, b: lambda e: e.tensor_reduce(out=Omall[:, b, :], in_=tm_[:, :].rearrange("p (h d) -> p d h", d=64), axis=AX.X, op=ALU.add))(tm_, b),
                         reads=[btm_], writes=[bOmall])

                def interleave(gens):
                    gens = list(gens)
                    while gens:
                        for g in list(gens):
                            try:
                                next(g)
                            except StopIteration:
                                gens.remove(g)

                for (tt_, bb__) in (Kt[1], Vt[1], Vt[2]):
                    P.op("pool", (lambda tt_: lambda e: e.memset(tt_[64:128, 0:4, :], 0.0))(tt_), writes=[bb__])
                for hh in range(2):
                    P.op("pool", (lambda hh: lambda e: e.dma_start(
                        out=Wo[:, hh * 4:(hh + 1) * 4, :],
                        in_=w_out[hh * 512:(hh + 1) * 512, :].rearrange("(k p) n -> p k n", p=128)))(hh),
                         writes=[bWo], dma=bWo)
                P.op("sp", lambda e: e.dma_start(out=gfin[:], in_=gfin_d), writes=[bgfin], dma=bgfin)
                load_k(1); load_v(1); load_v(2)
                interleave([stageA(0)])
                for b in range(NB):
                    if b + 2 < NB:
                        load_k(b + 2)
                    gens = [stageB(b), out_tile(xo[b * 128:(b + 1) * 128, :], y_d[b * 128:(b + 1) * 128, :], mix_prompt,
                                                [bmixA, bmixB], slice(b * 128, (b + 1) * 128), 128, b)]
                    if b + 1 < NB:
                        gens.insert(0, stageA(b + 1))
                    interleave(gens)
                    if b + 3 < NB:
                        load_v(b + 3)
                P.op("dve", lambda e: e.tensor_tensor(out=L1[:, :], in0=L1[:, :], in1=L2[:, :], op=ALU.add), reads=[bL2], writes=[bL1])
                P.op("dve", lambda e: e.reciprocal(out=L1[:, :], in_=L1[:, :]), writes=[bL1])
                P.op("dve", lambda e: e.reciprocal(out=Lm[:, :], in_=Lm[:, :]), writes=[bLm])
                L1b = sub_ap(L1[:, 0:1], [[1, NB], [0, 64]])
                Lmb = sub_ap(Lm[:, 0:1], [[1, NB], [0, 64]])
                for k in range(2):
                    P.op("dve", (lambda k: lambda e: e.scalar_tensor_tensor(out=Opad[:, :, k * 64:(k + 1) * 64], in0=Oall[:, :, :], scalar=par[:, k:k + 1],
                                                                            in1=L1b, op0=ALU.mult, op1=ALU.mult))(k),
                         reads=[bOall, bL1, bpar], writes=[bOpad])
                    P.op("dve", (lambda k: lambda e: e.scalar_tensor_tensor(out=Ompad[:, :, k * 64:(k + 1) * 64], in0=Omall[:, :, :], scalar=par[0:16, k:k + 1],
                                                                            in1=Lmb, op0=ALU.mult, op1=ALU.mult))(k),
                         reads=[bOmall, bLm, bpar], writes=[bOmpad])
                bT, bbT = next_bank(); bTm, bbTm = next_bank()
                for b in range(NB):
                    P.op("pe", (lambda b: lambda e: e.transpose(bT[:, b * 32:(b + 1) * 32], Opad[0:32, b, :], ident_f[0:32, 0:32]))(b),
                         reads=[bOpad, bidf], writes=[bbT])
                    P.op("pe", (lambda b: lambda e: e.transpose(bTm[:, b * 16:(b + 1) * 16], Ompad[0:16, b, :], ident_f[0:16, 0:16]))(b),
                         reads=[bOmpad, bidf], writes=[bbTm])
                for c in range(4):
                    for half in range(2):
                        h = 2 * c + half
                        r0, r1 = half * 64, half * 64 + 64
                        P.op("dve", (lambda c, h, r0, r1: lambda e: e.tensor_tensor(
                            out=mixS[r0:r1, 2 + c, :].rearrange("p (b j) -> p b j", j=4),
                            in0=bT[r0:r1, :].rearrange("p (b x) -> p b x", x=32)[:, :, h * 4:h * 4 + 4],
                            in1=szS[r0:r1, 2 + c, :].rearrange("p (b j) -> p b j", j=4), op=ALU.mult))(c, h, r0, r1),
                             reads=[bszS], writes=[bbT, bmixS])
                for cm in range(2):
                    for half in range(2):
                        h = 2 * cm + half
                        r0, r1 = half * 64, half * 64 + 64
                        P.op("dve", (lambda cm, h, r0, r1: lambda e: e.tensor_tensor(
                            out=mixS[r0:r1, 6 + cm, :].rearrange("p (b j) -> p b j", j=4),
                            in0=bTm[r0:r1, 0:256].rearrange("p (b x) -> p b x", x=16)[:, :, h * 4:h * 4 + 4],
                            in1=szS[r0:r1, 6 + cm, :].rearrange("p (b j) -> p b j", j=4), op=ALU.mult))(cm, h, r0, r1),
                             reads=[bszS], writes=[bbTm, bmixS])
                interleave([out_tile(xs_in[:, :], ys_d[:, :], lambda c, tsl: mixS[:, c, tsl], [bmixS], slice(0, NS), NS, 16)])

    P.emit()
    return nc


def _const_tables():
    slopes = np.exp2(-8.0 * (np.arange(8, dtype=np.float64) + 1.0) / 8.0)
    p = np.arange(128)[:, None]
    f = np.arange(128)[None, :]
    btab = np.zeros((4, 128, 3, 2, 512), np.float32)
    for pair in range(4):
        for ci, dil in enumerate(DILS):
            for hp in range(2):
                s = slopes[pair * 2 + hp] * dil
                da = f - p + 128
                A = np.where(f <= p, -s * da, NEG)
                db = f - p
                B = np.where(f >= p, -s * db, NEG)
                blk = np.concatenate([A, B], axis=1)
                btab[pair, :, ci, hp, :] = 8.0 * np.concatenate([blk, blk], axis=1)
    wmain = np.zeros((32, 2, 4, 128), np.float64)
    wnew = np.zeros((32, NB, NB, 4), np.float64)
    bmask = np.zeros((32, 8, 64), np.float32)
    bmaskm = np.zeros((16, 4, 64), np.float32)
    par = np.zeros((32, 2), np.float32)
    m = np.arange(128)
    for h in range(8):
        s = slopes[h]
        for j in range(4):
            pidx = h * 4 + j
            bmask[pidx, h, :] = 1.0
            par[pidx, h % 2] = 1.0
            if h < 4:
                bmaskm[pidx, h, :] = 1.0
            wmain[pidx, 0, j, :] += np.where(m < 96, np.exp(-s * 16.0 * (128 - m)), 0.0)
            for mm in range(96, 128):
                wmain[pidx, 1, j, 4 * (mm - 96)] += np.exp(-s * 16.0 * (128 - mm))
            wmain[pidx, 1, j, :] += np.exp(-s * 4.0 * (128 - m))
            for jp in range(4):
                kd = 512 + j - 4 * m - jp
                ok = (kd > j) & (kd <= 128)
                wmain[pidx, 1, jp, :] += np.where(ok, np.exp(-s * 1.0 * np.clip(kd, 0, 200)), 0.0)
            for b in range(NB):
                for jp in range(j + 1):
                    wnew[pidx, b, b, jp] += np.exp(-s * (j - jp))
                wnew[pidx, b, b, j] += 2.0
    return dict(btab=btab.reshape(4, 128, 3 * 2 * 512), slopes=slopes,
                wmain=wmain.reshape(32, 1024).astype(np.float32), wnew=wnew.reshape(32, NB * 64).astype(np.float32),
                bmask=bmask.reshape(32, 512), bmaskm=bmaskm.reshape(16, 256), par=par)


_CACHE = {}


def kernel(x_prompt, x_sample, mem_prompt, cache_win_k, cache_win_v, cache_conv, cache_mem_k, cache_mem_v,
           g_in, w_in, conv_w, g_mem, w_mem_kv, w_out, g_final):
    f32 = np.float32
    if "nc" not in _CACHE:
        _CACHE["nc"] = build_program()
        _CACHE["tabs"] = _const_tables()
    nc = _CACHE["nc"]
    tabs = _CACHE["tabs"]
    xp = np.asarray(x_prompt, f32)
    xsamp = np.asarray(x_sample, f32).reshape(128 * 4, D)
    cwk = np.asarray(cache_win_k, f32).reshape(128, 2048, 512)
    cwv = np.asarray(cache_win_v, f32).reshape(128, 2048, 512)
    ccv = np.asarray(cache_conv, f32).reshape(128, 2, 256)
    cmk = np.asarray(cache_mem_k, f32).reshape(128, 256, 256)
    cmv = np.asarray(cache_mem_v, f32).reshape(128, 256, 256)
    rep = lambda v: np.ascontiguousarray(np.broadcast_to(np.asarray(v, f32).reshape(1, D), (128, D)))
    convw = np.ascontiguousarray(np.asarray(conv_w, f32).reshape(3, 2, 128).transpose(2, 1, 0))
    ident = np.eye(128, dtype=f32)
    zeros_h = np.zeros((TH, D), f32)
    shared = dict(w_in=np.ascontiguousarray(np.asarray(w_in, f32).reshape(D, DIN)),
                  w_mem=np.ascontiguousarray(np.asarray(w_mem_kv, f32).reshape(D, 512)),
                  w_out=np.ascontiguousarray(np.asarray(w_out, f32).reshape(D, D)),
                  gin=rep(g_in), gmem=rep(g_mem), gfin=rep(g_final), convw=convw, ident=ident,
                  btab=tabs["btab"],
                  wmain=tabs["wmain"], wnew=tabs["wnew"], bmask=tabs["bmask"], bmaskm=tabs["bmaskm"], par=tabs["par"])
    in_maps = []
    for c in range(NCORES):
        b, ch = divmod(c, 4)
        m = dict(shared)
        m["xo"] = np.ascontiguousarray(xp[b, ch * T:(ch + 1) * T])
        m["xh"] = np.ascontiguousarray(xp[b, (ch - 1) * T:ch * T]) if ch > 0 else zeros_h
        m["hflag"] = np.full((128, 128), 1.0 if ch > 0 else 0.0, f32)
        m["mem"] = np.ascontiguousarray(np.asarray(mem_prompt, f32)[b])
        sl = slice(c * NB, (c + 1) * NB)
        m["xs"] = np.ascontiguousarray(xsamp[c * NS:(c + 1) * NS])
        m["cwk"] = np.ascontiguousarray(cwk[sl]); m["cwv"] = np.ascontiguousarray(cwv[sl])
        m["ccv"] = np.ascontiguousarray(ccv[sl].reshape(NB * 2, 256))
        m["cmk"] = np.ascontiguousarray(cmk[sl]); m["cmv"] = np.ascontiguousarray(cmv[sl])
        in_maps.append(m)
    res = run_bass_kernel_spmd(nc, in_maps, core_ids=list(range(NCORES)))
    R = res.results
    y_prompt = np.stack([np.concatenate([R[b * 4 + ch]["y"] for ch in range(4)], axis=0) for b in range(2)], 0)
    y_sample = np.concatenate([R[c]["ys"] for c in range(NCORES)], 0).reshape(128, 4, D)
    p_wk = np.stack([R[b * 4 + 3]["pwk"] for b in range(2)], 0).reshape(1, 2, 2048, 8, 64)
    p_wv = np.stack([R[b * 4 + 3]["pwv"] for b in range(2)], 0).reshape(1, 2, 2048, 8, 64)
    p_cv = np.stack([R[b * 4 + 3]["pconv"] for b in range(2)], 0).reshape(1, 2, 2, 256)
    p_mk = np.stack([R[b * 4]["pmk"] for b in range(2)], 0).reshape(1, 2, 256, 4, 64)
    p_mv = np.stack([R[b * 4]["pmv"] for b in range(2)], 0).reshape(1, 2, 256, 4, 64)
    s_wk = np.concatenate([R[c]["swk"] for c in range(NCORES)], 0).reshape(1, 128, 4, 8, 64)
    s_wv = np.concatenate([R[c]["swv"] for c in range(NCORES)], 0).reshape(1, 128, 4, 8, 64)
    s_cv = np.concatenate([R[c]["sconv"] for c in range(NCORES)], 0).reshape(1, 128, 2, 256)
    outs = (y_prompt, y_sample, p_wk, p_wv, p_cv, p_mk, p_mv, s_wk, s_wv, s_cv)
    return tuple(np.ascontiguousarray(o, dtype=f32) for o in outs)
```

```python
import contextlib
import numpy as np
import ml_dtypes
import concourse.bass as bass
import concourse.mybir as mybir
from concourse.bass_utils import run_bass_kernel_spmd

F32 = mybir.dt.float32
BF16 = mybir.dt.bfloat16
AF = mybir.ActivationFunctionType
ALU = mybir.AluOpType
AX = mybir.AxisListType

NCORES = 8
D = 1024
T = 2048
TH = 2048
DIN = 3584
NS = 64
NB = 16
EPS = 1e-6
DILS = (1, 4, 16)
NEG = -30000.0

STAGE = 9


class Buf:
    __slots__ = ("name", "writers", "readers", "dsem", "dcnt")

    def __init__(self, name):
        self.name = name
        self.writers = []
        self.readers = []
        self.dsem = None
        self.dcnt = 0


class Prog:
    ENGS = ("pe", "act", "dve", "pool", "sp")

    def __init__(self, nc):
        self.nc = nc
        self.ops = []
        self.last = {}
        self.dma_last = {}
        self.out_dmas = []

    def _prune(self, lst, i):
        r = self.ops[i]
        out = []
        for j in lst:
            o = self.ops[j]
            if r["dma"] is None and o["dma"] is None and o["eng"] == r["eng"]:
                continue
            if r["dma"] is not None and o["dma"] is r["dma"]:
                continue
            out.append(j)
        out.append(i)
        return out

    def op(self, eng, fn, reads=(), writes=(), dma=None, deps=(), out=False):
        i = len(self.ops)
        d = set(deps)
        for b in reads:
            d.update(b.writers)
        for b in writes:
            for w in b.writers:
                if dma is not None and self.ops[w]["dma"] is dma:
                    continue
                d.add(w)
            d.update(b.readers)
        self.ops.append(dict(eng=eng, fn=fn, deps=sorted(d), dma=dma, sig=False, tok=None))
        for b in writes:
            if b.readers:
                b.writers = [i]
                b.readers = []
            else:
                b.writers = self._prune(b.writers, i)
        for b in reads:
            b.readers = self._prune(b.readers, i)
        if dma is None:
            self.last[eng] = i
        else:
            self.dma_last[dma.name] = i
            if out:
                self.out_dmas.append(i)
        return i

    def barrier(self, scratch):
        deps = list(self.last.values()) + list(self.dma_last.values())
        ids = []
        for k, eng in enumerate(("act", "dve", "pool")):
            ap = scratch[0:1, k:k + 1]
            if eng == "act":
                ap2 = scratch[0:1, 4:5]
                ids.append(self.op(eng, (lambda a, a2: (lambda e: e.activation(out=a, in_=a2, func=AF.Copy)))(ap, ap2), deps=deps))
            else:
                ids.append(self.op(eng, (lambda a: (lambda e: e.memset(a, 0.0)))(ap), deps=deps))
        return ids

    def emit(self, final_eng="sp"):
        nc = self.nc
        ops = self.ops
        for r in ops:
            for dd in r["deps"]:
                ops[dd]["sig"] = True
        esem = {e: nc.alloc_semaphore("sem_" + e) for e in self.ENGS}
        cnt = {e: 0 for e in self.ENGS}
        for r in ops:
            if r["dma"] is not None:
                b = r["dma"]
                if b.dsem is None:
                    b.dsem = nc.alloc_semaphore("dsem_" + b.name)
                b.dcnt += 1
                r["tok"] = (b.dsem, 16 * b.dcnt)
            elif r["sig"]:
                cnt[r["eng"]] += 1
                r["tok"] = (esem[r["eng"]], cnt[r["eng"]])
        outs = list(self.out_dmas)

        def run(engname):
            def f(eng):
                waited = {}

                def wait_tok(tok):
                    sem, val = tok
                    if waited.get(sem.num, 0) >= val:
                        return
                    eng.wait_ge(sem, val)
                    waited[sem.num] = val

                for r in ops:
                    if r["eng"] != engname:
                        continue
                    need = {}
                    for dd in r["deps"]:
                        o = ops[dd]
                        if engname == "pe" and o["eng"] == "pe" and o["dma"] is None:
                            continue
                        sem, val = o["tok"]
                        if need.get(sem.num, (None, 0))[1] < val:
                            need[sem.num] = (sem, val)
                    for sem, val in need.values():
                        wait_tok((sem, val))
                    ins = r["fn"](eng)
                    if r["tok"] is not None:
                        ins.then_inc(r["tok"][0], 16 if r["dma"] is not None else 1)
                if engname == final_eng:
                    for i in outs:
                        wait_tok(ops[i]["tok"])
            return f

        with nc.Block() as block:
            block.sync(run("sp"))
            block.scalar(run("act"))
            block.vector(run("dve"))
            block.gpsimd(run("pool"))
            block.tensor(run("pe"))


def sub_ap(base, extra):
    return bass.AP(tensor=base.tensor, offset=base.offset, ap=[list(base.ap[0])] + [list(x) for x in extra])


def build_program():
    nc = bass.Bass("TRN2", target_bir_lowering=False)
    P = Prog(nc)

    def din(name, shape, dt=F32):
        return nc.dram_tensor(name, list(shape), dt, kind="ExternalInput").ap()

    def dout(name, shape, dt=F32):
        return nc.dram_tensor(name, list(shape), dt, kind="ExternalOutput").ap()

    xo = din("xo", [T, D]); xh = din("xh", [TH, D]); mem = din("mem", [256, D]); xs_in = din("xs", [NS, D])
    cwk = din("cwk", [NB, 2048, 512]); cwv = din("cwv", [NB, 2048, 512])
    ccv = din("ccv", [NB * 2, 256]); cmk = din("cmk", [NB, 256, 256]); cmv = din("cmv", [NB, 256, 256])
    w_in = din("w_in", [D, DIN]); w_mem = din("w_mem", [D, 512]); w_out = din("w_out", [D, D])
    gin_d = din("gin", [128, D]); gmem_d = din("gmem", [128, D]); gfin_d = din("gfin", [128, D])
    convw_d = din("convw", [128, 2, 3]); ident_d = din("ident", [128, 128]); hflag_d = din("hflag", [128, 128])
    btab_d = din("btab", [4, 128, 3 * 2 * 512])
    wmain_d = din("wmain", [32, 1024]); wnew_d = din("wnew", [32, NB * 64])
    bmask_d = din("bmask", [32, 512]); bmaskm_d = din("bmaskm", [16, 256]); par_d = din("par", [32, 2])

    y_d = dout("y", [T, D]); ys_d = dout("ys", [NS, D])
    pwk_d = dout("pwk", [T, 512]); pwv_d = dout("pwv", [T, 512]); pconv_d = dout("pconv", [2, 256])
    pmk_d = dout("pmk", [256, 256]); pmv_d = dout("pmv", [256, 256])
    swk_d = dout("swk", [NS, 512]); swv_d = dout("swv", [NS, 512]); sconv_d = dout("sconv", [NB, 2, 256])
    vs_d = nc.dram_tensor("vscratch", [4, TH + T, 128], BF16).ap()

    es = contextlib.ExitStack()
    bufs = {}

    stack_bufs = {}
    freed = set()

    def sb(name, shape, dt, stack=None):
        t = (stack or es).enter_context(nc.sbuf_tensor("s_" + name, list(shape), dt))
        bufs[name] = Buf(name)
        stack_bufs.setdefault(id(stack or es), []).append(bufs[name])
        return t, bufs[name]

    def collect(stack, extra=()):
        for b in stack_bufs.get(id(stack), []) + list(extra):
            freed.update(b.writers)
            freed.update(b.readers)

    with es:
        banks = []
        for i in range(8):
            t = es.enter_context(nc.psum_tensor("bank%d" % i, [128, 512], F32))
            banks.append((t, Buf("bank%d" % i)))
        bank_rr = {"main": 0, "aux": 0}
        bank_pool = {"main": list(range(8)), "aux": [6, 7]}

        def next_bank(pool="main"):
            lst = bank_pool[pool]
            b = banks[lst[bank_rr[pool] % len(lst)]]
            bank_rr[pool] += 1
            return b

        def bf(bank_t):
            return bank_t[:].bitcast(BF16)

        ident_f, bidf = sb("ident_f", [128, 128], F32)
        ident_b, bidb = sb("ident_b", [128, 128], BF16)
        ones_b, bones = sb("ones_b", [128, 128], BF16)
        hones_f, bhof = sb("hones_f", [128, 128], F32)
        hones_b, bhob = sb("hones_b", [128, 128], BF16)
        scr, bscr = sb("scr", [128, 8], F32)
        epsb, bepsb = sb("epsb", [128, 1], F32)
        mixA, bmixA = sb("mixA", [128, 4, T], BF16)
        mixB, bmixB = sb("mixB", [128, 4, T], BF16)
        mkT, bmkT = sb("mkT", [128, 2, 256], BF16)
        mvb, bmvb = sb("mvb", [128, 2, 256], BF16)
        mixS, bmixS = sb("mixS", [128, 8, NS], BF16)
        stats, _bstats0 = sb("stats", [128, 8], F32)
        bstatc = [Buf("stats%d" % i) for i in range(8)]
        hS, bhS = sb("hS", [128, 8, NS], BF16)
        qS, bqS = sb("qS", [128, 4, NS], BF16)
        kS, bkS = sb("kS", [128, 4, NS], BF16)
        mqS, bmqS = sb("mqS", [128, 2, NS], BF16)
        szS, bszS = sb("szS", [128, 8, NS], F32)
        vSb, bvSb = sb("vSb", [NS, 512], BF16)
        uS, buS = sb("uS", [128, 2, NB, 6], F32)
        cbS, bcbS = sb("cbS", [128, 2, NS], F32)
        pk = contextlib.ExitStack()
        kT, bkT = sb("kT", [128, 4, TH + T], BF16, pk)
        qT, bqT = sb("qT", [128, 4, T], BF16, pk)
        szA, bszA = sb("szA", [128, 4, T], BF16, pk)
        gin, bgin = sb("gin_t", [128, D], F32, pk)

        c0 = P.op("sp", lambda e: e.dma_start(out=gin[:], in_=gin_d), writes=[bgin], dma=bgin)
        P.op("sp", lambda e: e.dma_start(out=ident_f[:], in_=ident_d), writes=[bidf], dma=bidf)
        P.op("sp", lambda e: e.dma_start(out=hones_f[:], in_=hflag_d), writes=[bhof], dma=bhof)
        P.op("dve", lambda e: e.tensor_copy(out=ident_b[:], in_=ident_f[:]), reads=[bidf], writes=[bidb])
        P.op("dve", lambda e: e.tensor_copy(out=hones_b[:], in_=hones_f[:]), reads=[bhof], writes=[bhob])
        P.op("pool", lambda e: e.memset(ones_b[:], 1.0), writes=[bones])
        P.op("pool", lambda e: e.memset(epsb[:], EPS), writes=[bepsb])

        def rms_scale(x_t, bx, n, g_t, bg, xs_t, bxs, col):
            bstats = bstatc[col]
            P.op("act", lambda e: e.activation(out=xs_t[0:n, :], in_=x_t[0:n, :], func=AF.Square,
                                               accum_out=stats[0:n, col:col + 1]),
                 reads=[bx], writes=[bstats, bxs])
            P.op("act", lambda e: e.activation(out=stats[0:n, col:col + 1], in_=stats[0:n, col:col + 1], func=AF.Ln,
                                               scale=1.0 / D, bias=epsb[0:n, 0:1]),
                 reads=[bstats, bepsb], writes=[bstats])
            P.op("act", lambda e: e.activation(out=stats[0:n, col:col + 1], in_=stats[0:n, col:col + 1], func=AF.Exp, scale=-0.5),
                 reads=[bstats], writes=[bstats])
            P.op("dve", lambda e: e.scalar_tensor_tensor(out=xs_t[0:n, :], in0=x_t[0:n, :],
                                                         scalar=stats[0:n, col:col + 1], in1=g_t[0:n, :],
                                                         op0=ALU.mult, op1=ALU.mult),
                 reads=[bx, bstats, bg], writes=[bxs])

        def transpose_to(xs_t, bxs, n, dst_fn, bdst, evac_eng):
            bt, bb = next_bank()
            v = bf(bt).rearrange("p (k t) -> p k t", t=128)
            for kc in range(8):
                P.op("pe", (lambda kc: lambda e: e.transpose(v[:, kc, 0:n], xs_t[0:n, kc * 128:(kc + 1) * 128],
                                                              ident_b[0:n, 0:n]))(kc),
                     reads=[bxs, bidb], writes=[bb])
            dst = dst_fn()
            if evac_eng == "act":
                P.op("act", lambda e: e.activation(out=dst, in_=v[:, :, 0:n], func=AF.Copy), writes=[bb, bdst])
            else:
                P.op("dve", lambda e: e.tensor_copy(out=dst, in_=v[:, :, 0:n]), writes=[bb, bdst])

        def proj_fm(W_t, bW, col0, hT_ap, bh, n):
            bt, bb = next_bank()
            for kc in range(8):
                P.op("pe", (lambda kc: lambda e: e.matmul(bt[:, 0:n], lhsT=W_t[:, kc, col0:col0 + 128],
                                                           rhs=hT_ap(kc), start=(kc == 0), stop=(kc == 7)))(kc),
                     reads=[bW, bh], writes=[bb])
            return bt, bb

        def proj_tm(W_t, bW, col0, ncol, hT_ap, bh, ntok=128):
            bt, bb = next_bank()
            for kc in range(8):
                P.op("pe", (lambda kc: lambda e: e.matmul(bt[0:ntok, 0:ncol], lhsT=hT_ap(kc),
                                                           rhs=W_t[:, kc, col0:col0 + ncol],
                                                           start=(kc == 0), stop=(kc == 7)))(kc),
                     reads=[bW, bh], writes=[bb])
            return bt, bb

        pAB = contextlib.ExitStack()
        NXB = 2
        xt = [sb("xt%d" % i, [128, D], F32, pAB) for i in range(NXB)]
        xsb = [sb("xsb%d" % i, [128, D], BF16, pAB) for i in range(4)]
        hTb = [sb("hT%d" % i, [128, 8, 512], BF16, pAB) for i in range(2)]
        pr = contextlib.ExitStack()
        WrA = pr.enter_context(nc.sbuf_tensor("s_WrA", [128, 8, 1024], BF16, side="right"))
        bWrA = Buf("WrA")
        ph = contextlib.ExitStack()
        with ph:
            Wi, bWi = sb("Wkv", [128, 8, 1024], BF16, ph)
            Wm, bWm = sb("Wm", [128, 8, 512], BF16, ph)
            gmem, bgmem = sb("gmem_t", [128, D], F32, ph)
            P.op("pool", lambda e: e.dma_start(out=Wm[:], in_=w_mem.rearrange("(k p) n -> p k n", p=128)),
                 writes=[bWm], dma=bWm)
            for kc in range(8):
                P.op("pool", (lambda kc, W: lambda e: e.dma_start(out=W[:, kc, :], in_=w_in[kc * 128:(kc + 1) * 128, 1280:2304]))(kc, Wi),
                     writes=[bWi], dma=bWi)
            P.op("sp", lambda e: e.dma_start(out=gmem[:], in_=gmem_d), writes=[bgmem], dma=bgmem)

            def prefetch_wra(kcs):
                for kc in kcs:
                    P.op("pool", (lambda kc: lambda e: e.dma_start(out=WrA[:, kc, 0:512], in_=w_in[kc * 128:(kc + 1) * 128, 768:1280]))(kc),
                         writes=[bWrA], dma=bWrA)
                    P.op("pool", (lambda kc: lambda e: e.dma_start(out=WrA[:, kc, 512:1024], in_=w_in[kc * 128:(kc + 1) * 128, 2816:3328]))(kc),
                         writes=[bWrA], dma=bWrA)

            kvst = [sb("kvst%d" % i, [128, 1024], F32, ph) for i in range(2)]
            vbst = [sb("vbst%d" % i, [128, 512], BF16, ph) for i in range(2)]
            bvs = Buf("vscratch")
            cnt = {"x": 0, "xs": 0, "kv": 0, "vb": 0, "pm": 0}

            for mt in range(2):
                x_t, bx = xt[cnt["x"] % NXB]; cnt["x"] += 1
                P.op("sp", (lambda mt, x_t: lambda e: e.dma_start(out=x_t[:], in_=mem[mt * 128:(mt + 1) * 128, :]))(mt, x_t),
                     writes=[bx], dma=bx)
                s_t, bs = xsb[cnt["xs"] % len(xsb)]; cnt["xs"] += 1
                rms_scale(x_t, bx, 128, gmem, bgmem, s_t, bs, 0)
                h_t, bh = hTb[0]
                transpose_to(s_t, bs, 128, (lambda mt, h_t: lambda: h_t[:, :, mt * 128:(mt + 1) * 128])(mt, h_t), bh, "act")
            h_t, bh = hTb[0]
            for cm in range(2):
                bt, bb = proj_fm(Wm, bWm, cm * 128, (lambda h_t: lambda kc: h_t[:, kc, 0:256])(h_t), bh, 256)
                P.op("act", (lambda cm, bt: lambda e: e.activation(out=mkT[:, cm, :], in_=bt[:, 0:256], func=AF.Copy))(cm, bt),
                     writes=[bb, bmkT])
            for mt in range(2):
                bt, bb = proj_tm(Wm, bWm, 0, 512, (lambda mt, h_t: lambda kc: h_t[:, kc, mt * 128:(mt + 1) * 128])(mt, h_t), bh)
                st, bst = kvst[cnt["kv"] % 2]; cnt["kv"] += 1
                P.op("dve", (lambda st, bt: lambda e: e.tensor_copy(out=st[:, 0:512], in_=bt[:, 0:512]))(st, bt),
                     writes=[bb, bst])
                P.op("pool", (lambda st, mt: lambda e: e.tensor_copy(out=mvb[:, mt, :], in_=st[:, 256:512]))(st, mt),
                     reads=[bst], writes=[bmvb])
                P.op("pool", (lambda st, mt: lambda e: e.dma_start(out=pmk_d[mt * 128:(mt + 1) * 128, :], in_=st[:, 0:256]))(st, mt),
                     reads=[bst], dma=bst, out=True)
                P.op("pool", (lambda st, mt: lambda e: e.dma_start(out=pmv_d[mt * 128:(mt + 1) * 128, :], in_=st[:, 256:512]))(st, mt),
                     reads=[bst], dma=bst, out=True)

            def norm_part1(src, blk):
                res = []
                for tl in range(4):
                    x_t, bx = xt[cnt["x"] % NXB]; cnt["x"] += 1
                    r0 = blk * 512 + tl * 128
                    P.op("sp", (lambda x_t, r0: lambda e: e.dma_start(out=x_t[:], in_=src[r0:r0 + 128, :]))(x_t, r0),
                         writes=[bx], dma=bx)
                    s_t, bs = xsb[cnt["xs"] % len(xsb)]; cnt["xs"] += 1
                    rms_scale(x_t, bx, 128, gin, bgin, s_t, bs, 1 + (tl % 2))
                    res.append((s_t, bs))
                return res

            def norm_part2(res, h_t, bh):
                for tl, (s_t, bs) in enumerate(res):
                    transpose_to(s_t, bs, 128, (lambda tl, h_t: lambda: h_t[:, :, tl * 128:(tl + 1) * 128])(tl, h_t), bh,
                                 "act" if tl % 2 == 0 else "dve")

            def v_tokmajor(h_t, bh, tok0, own):
                for tl in range(4):
                    hap = (lambda tl: lambda kc: h_t[:, kc, tl * 128:(tl + 1) * 128])(tl)
                    vb, bvb = vbst[cnt["vb"] % 2]; cnt["vb"] += 1
                    r0 = tok0 + tl * 128
                    if own:
                        st, bst = kvst[cnt["kv"] % 2]; cnt["kv"] += 1
                        bt, bb = proj_tm(Wi, bWi, 0, 512, hap, bh)
                        P.op("act", (lambda st, bt: lambda e: e.activation(out=st[:, 0:512], in_=bt[:, 0:512], func=AF.Copy))(st, bt),
                             writes=[bb, bst])
                    bt2, bb2 = proj_tm(Wi, bWi, 512, 512, hap, bh)
                    P.op("dve", (lambda vb, bt2: lambda e: e.tensor_copy(out=vb[:], in_=bt2[:, 0:512]))(vb, bt2),
                         writes=[bb2, bvb])
                    if own:
                        P.op("act", (lambda st, bt2: lambda e: e.activation(out=st[:, 512:1024], in_=bt2[:, 0:512], func=AF.Copy))(st, bt2),
                             writes=[bb2, bst])
                    P.op("pool", (lambda vb, r0: lambda e: e.dma_start(
                        out=vs_d.rearrange("q t c -> t q c")[r0:r0 + 128],
                        in_=vb[:].rearrange("p (q c) -> p q c", c=128)))(vb, r0),
                         reads=[bvb], writes=[bvs], dma=bvb)
                    if own:
                        o0 = r0 - TH
                        P.op("pool", (lambda st, o0: lambda e: e.dma_start(out=pwk_d[o0:o0 + 128, :], in_=st[:, 0:512]))(st, o0),
                             reads=[bst], dma=bst, out=True)
                        P.op("pool", (lambda st, o0: lambda e: e.dma_start(out=pwv_d[o0:o0 + 128, :], in_=st[:, 512:1024]))(st, o0),
                             reads=[bst], dma=bst, out=True)

            hapS = lambda kc: hS[:, kc, :]

            def sample_kv():
                x_t, bxS = xt[cnt["x"] % NXB]; cnt["x"] += 1
                xS = x_t
                P.op("sp", lambda e: e.dma_start(out=xS[0:NS, :], in_=xs_in), writes=[bxS], dma=bxS)
                s_t, bs = xsb[cnt["xs"] % len(xsb)]; cnt["xs"] += 1
                rms_scale(xS, bxS, NS, gin, bgin, s_t, bs, 3)
                transpose_to(s_t, bs, NS, lambda: hS[:, :, :], bhS, "act")
                hapS = lambda kc: hS[:, kc, :]
                for c in range(4):
                    bt, bb = proj_fm(Wi, bWi, c * 128, hapS, bhS, NS)
                    P.op("act", (lambda c, bt: lambda e: e.activation(out=kS[:, c, :], in_=bt[:, 0:NS], func=AF.Copy))(c, bt), writes=[bb, bkS])
                st, bst = kvst[cnt["kv"] % 2]; cnt["kv"] += 1
                bt, bb = proj_tm(Wi, bWi, 0, 512, hapS, bhS, NS)
                P.op("act", (lambda st, bt: lambda e: e.activation(out=st[0:NS, 0:512], in_=bt[0:NS, 0:512], func=AF.Copy))(st, bt), writes=[bb, bst])
                bt2, bb2 = proj_tm(Wi, bWi, 512, 512, hapS, bhS, NS)
                P.op("dve", (lambda st, bt2: lambda e: e.tensor_copy(out=st[0:NS, 512:1024], in_=bt2[0:NS, 0:512]))(st, bt2), writes=[bb2, bst])
                P.op("pool", (lambda st: lambda e: e.tensor_copy(out=vSb[:, :], in_=st[0:NS, 512:1024]))(st), reads=[bst], writes=[bvSb])
                P.op("pool", (lambda st: lambda e: e.dma_start(out=swk_d[:, :], in_=st[0:NS, 0:512]))(st), reads=[bst], dma=bst, out=True)
                P.op("pool", (lambda st: lambda e: e.dma_start(out=swv_d[:, :], in_=st[0:NS, 512:1024]))(st), reads=[bst], dma=bst, out=True)

            norm_part2(norm_part1(xh, 0), hTb[0][0], hTb[0][1])
            for blk in range(8):
                own = blk >= 4
                h_t, bh = hTb[blk % 2]
                nres = None
                if blk + 1 < 8:
                    nres = norm_part1(xo if blk + 1 >= 4 else xh, (blk + 1) % 4)
                tok0 = blk * 512
                if blk == 7:
                    sample_kv()
                if 2 <= blk <= 5:
                    prefetch_wra([2 * (blk - 2), 2 * (blk - 2) + 1])
                hap512 = (lambda h_t: lambda kc: h_t[:, kc, :])(h_t)
                for c in range(4):
                    bt, bb = proj_fm(Wi, bWi, c * 128, hap512, bh, 512)
                    P.op("act", (lambda c, bt, tok0: lambda e: e.activation(out=kT[:, c, tok0:tok0 + 512], in_=bt[:, :], func=AF.Copy))(c, bt, tok0),
                         writes=[bb, bkT])
                v_tokmajor(h_t, bh, tok0, own)
                if nres is not None:
                    norm_part2(nres, hTb[(blk + 1) % 2][0], hTb[(blk + 1) % 2][1])

        collect(ph)
        bar = sorted(freed)

        pbn = [0]

        def phase_buf(name, shape, dt, stack):
            pbn[0] += 1
            t, b = sb("%s_p%d" % (name, pbn[0]), shape, dt, stack)
            b.readers = list(bar)
            return t, b

        ph = contextlib.ExitStack()
        with ph:
            WrB, bWconv = phase_buf("WrB", [128, 8, 1536], BF16, ph)
            bWmem = Buf("WrBmem")
            bWmem.readers = list(bar)
            convw, bconvw = phase_buf("convw_t", [128, 2, 3], F32, ph)
            for (d0, s0, n_, bw_) in ((256, 256, 512, bWconv), (1024, 2560, 256, bWconv), (0, 0, 256, bWconv),
                                      (768, 2304, 256, bWmem), (1280, 3328, 256, bWmem)):
                for k0 in (0, 4):
                    P.op("pool", (lambda k0, d0, s0, n_: lambda e: e.dma_start(
                        out=WrB[:, k0:k0 + 4, d0:d0 + n_],
                        in_=w_in[k0 * 128:(k0 + 4) * 128, s0:s0 + n_].rearrange("(k p) n -> p k n", p=128)))(k0, d0, s0, n_),
                         writes=[bw_], dma=bw_)
            P.op("sp", lambda e: e.dma_start(out=convw[:], in_=convw_d), writes=[bconvw], dma=bconvw)
            CB, CC, CH, CQ, CMQ, CZ = 0, 256, 512, 768, 1280, 1536

            def wsel(X):
                if CQ <= X < CMQ:
                    return WrA, bWrA, X - CQ
                if CZ + 256 <= X < CZ + 768:
                    return WrA, bWrA, 512 + X - (CZ + 256)
                if X < CQ:
                    return WrB, bWconv, X
                if CMQ <= X < CZ:
                    return WrB, bWmem, 768 + X - CMQ
                if CZ <= X < CZ + 256:
                    return WrB, bWconv, 1024 + X - CZ
                return WrB, bWmem, 1280 + X - (CZ + 768)
            ubuf, bubuf = phase_buf("ubuf", [128, 2, 2 + 512], F32, ph)
            ccs, bccs = phase_buf("ccs", [128, 512], F32, ph)
            tcv, btcv = phase_buf("tcv", [128, 512], F32, ph)
            szl, bszl = phase_buf("szl", [128, 512], F32, ph)
            mqT, bmqT = phase_buf("mqT", [128, 2, 512], BF16, ph)
            pmT = [phase_buf("pmT%d" % i, [128, 512], BF16, ph) for i in range(2)]
            rdn, brdn = phase_buf("rdn", [128, 512], F32, ph)
            utok, butok = rdn, brdn
            cnt = {"x": 0, "xs": 0, "pm": 0}

            P.op("pool", lambda e: e.memset(ubuf[:], 0.0), writes=[bubuf])

            def normB1(src, r0s):
                res = []
                for tl, r0 in enumerate(r0s):
                    x_t, bx = xt[cnt["x"] % NXB]; cnt["x"] += 1
                    P.op("sp", (lambda x_t, r0: lambda e: e.dma_start(out=x_t[:], in_=src[r0:r0 + 128, :]))(x_t, r0),
                         writes=[bx], dma=bx)
                    s_t, bs = xsb[cnt["xs"] % len(xsb)]; cnt["xs"] += 1
                    rms_scale(x_t, bx, 128, gin, bgin, s_t, bs, 1 + (tl % 2))
                    res.append((s_t, bs))
                return res

            def normB2(res, h_t, bh):
                for tl, (s_t, bs) in enumerate(res):
                    transpose_to(s_t, bs, 128, (lambda tl, h_t: lambda: h_t[:, :, tl * 128:(tl + 1) * 128])(tl, h_t), bh,
                                 "act" if tl % 2 == 0 else "dve")

            def load_norm_tiles(src, r0s, h_t, bh):
                normB2(normB1(src, r0s), h_t, bh)

            def conv_u(hap, bh, n, ub_ap_fn, bub):
                for ch in range(2):
                    btc, bbc = proj_fm(*wsel(CC + ch * 128), hap, bh, n)
                    P.op("act", (lambda btc: lambda e: e.activation(out=ccs[:, 0:n], in_=btc[:, 0:n], func=AF.Copy))(btc),
                         writes=[bbc, bccs])
                    bth, bbh = proj_fm(*wsel(CH + ch * 128), hap, bh, n)
                    P.op("dve", (lambda ch, bth: lambda e: e.tensor_tensor(out=ub_ap_fn(ch), in0=ccs[:, 0:n], in1=bth[:, 0:n],
                                                                           op=ALU.mult))(ch, bth),
                         reads=[bccs], writes=[bbh, bub])

            def sample_rest():
                for c in range(4):
                    bt, bb = proj_fm(*wsel(CQ + c * 128), hapS, bhS, NS)
                    P.op("act", (lambda c, bt: lambda e: e.activation(out=qS[:, c, :], in_=bt[:, 0:NS], func=AF.Copy))(c, bt), writes=[bb, bqS])
                for c in range(2):
                    bt, bb = proj_fm(*wsel(CMQ + c * 128), hapS, bhS, NS)
                    P.op("act", (lambda c, bt: lambda e: e.activation(out=mqS[:, c, :], in_=bt[:, 0:NS], func=AF.Copy))(c, bt), writes=[bb, bmqS])
                    bt, bb = proj_fm(*wsel(CB + c * 128), hapS, bhS, NS)
                    P.op("act", (lambda c, bt: lambda e: e.activation(out=cbS[:, c, :], in_=bt[:, 0:NS], func=AF.Copy))(c, bt), writes=[bb, bcbS])
                for c in range(8):
                    bt, bb = proj_fm(*wsel(CZ + c * 128), hapS, bhS, NS)
                    P.op("act", (lambda c, bt: lambda e: e.activation(out=szS[:, c, :], in_=bt[:, 0:NS], func=AF.Silu))(c, bt), writes=[bb, bszS])
                bt, bb = proj_tm(*wsel(CC), 512, hapS, bhS, NS)
                P.op("act", (lambda bt: lambda e: e.activation(out=utok[0:NS, 0:512], in_=bt[0:NS, 0:512], func=AF.Copy))(bt), writes=[bb, butok])
                P.op("pool", lambda e: e.tensor_tensor(out=utok[0:NS, 0:256], in0=utok[0:NS, 0:256], in1=utok[0:NS, 256:512], op=ALU.mult),
                     reads=[butok], writes=[butok])
                for b in range(NB):
                    P.op("pool", (lambda b: lambda e: e.dma_start(out=sconv_d[b], in_=utok[b * 4 + 2:b * 4 + 4, 0:256]))(b),
                         reads=[butok], dma=butok, out=True)
                if STAGE >= 3:
                    cct, bcct = phase_buf("cct", [32, 256], F32, ph)
                    P.op("sp", lambda e: e.dma_start(out=cct[:], in_=ccv), writes=[bcct], dma=bcct)
                    btt, bbt = next_bank()
                    for ch in range(2):
                        P.op("pe", (lambda ch: lambda e: e.transpose(btt[:, ch * 32:(ch + 1) * 32], cct[0:32, ch * 128:(ch + 1) * 128],
                                                                     ident_f[0:32, 0:32]))(ch),
                             reads=[bcct, bidf], writes=[bbt])
                    P.op("dve", lambda e: e.tensor_copy(out=uS[:, :, :, 0:2],
                                                        in_=btt[:, 0:64].rearrange("p (c b r) -> p c b r", c=2, r=2)),
                         writes=[bbt, buS])
                    for ch in range(2):
                        btc, bbc = proj_fm(*wsel(CC + ch * 128), hapS, bhS, NS)
                        P.op("act", (lambda btc: lambda e: e.activation(out=ccs[:, 0:NS], in_=btc[:, 0:NS], func=AF.Copy))(btc), writes=[bbc, bccs])
                        bth, bbh = proj_fm(*wsel(CH + ch * 128), hapS, bhS, NS)
                        P.op("dve", (lambda ch, bth: lambda e: e.tensor_tensor(
                            out=uS[:, ch, :, 2:6], in0=ccs[:, 0:NS].rearrange("p (b j) -> p b j", j=4),
                            in1=bth[:, 0:NS].rearrange("p (b j) -> p b j", j=4), op=ALU.mult))(ch, bth),
                             reads=[bccs], writes=[bbh, buS])
                        tv = tcv[:, 0:NS].rearrange("p (b j) -> p b j", j=4)
                        P.op("dve", (lambda ch, tv: lambda e: e.tensor_scalar(out=tv, in0=uS[:, ch, :, 0:4], scalar1=convw[:, ch, 0:1],
                                                                          scalar2=None, op0=ALU.mult))(ch, tv),
                             reads=[buS, bconvw], writes=[btcv])
                        for j in (1, 2):
                            P.op("dve", (lambda ch, j, tv: lambda e: e.scalar_tensor_tensor(
                                out=tv, in0=uS[:, ch, :, j:j + 4], scalar=convw[:, ch, j:j + 1], in1=tv, op0=ALU.mult, op1=ALU.add))(ch, j, tv),
                                 reads=[buS, bconvw, btcv], writes=[btcv])
                        P.op("dve", (lambda ch: lambda e: e.tensor_tensor(out=tcv[:, 0:NS], in0=tcv[:, 0:NS], in1=cbS[:, ch, :], op=ALU.mult))(ch),
                             reads=[btcv, bcbS], writes=[btcv])
                        P.op("dve", (lambda ch: lambda e: e.tensor_tensor(out=mixS[:, ch, :], in0=tcv[:, 0:NS], in1=szS[:, ch, :], op=ALU.mult))(ch),
                             reads=[btcv, bszS], writes=[bmixS])

            load_norm_tiles(xo, [tl * 128 for tl in range(4)], hTb[0][0], hTb[0][1])
            for blk in range(4):
                h_t, bh = hTb[blk % 2]
                nres = None
                if blk + 1 < 4 and blk != 0:
                    nres = normB1(xo, [(blk + 1) * 512 + tl * 128 for tl in range(4)])
                hap512 = (lambda h_t: lambda kc: h_t[:, kc, :])(h_t)
                o0 = blk * 512
                if blk == 3:
                    sample_rest()
                for c in range(4):
                    bt, bb = proj_fm(*wsel(CQ + c * 128), hap512, bh, 512)
                    P.op("dve", (lambda c, bt, o0: lambda e: e.tensor_copy(out=qT[:, c, o0:o0 + 512], in_=bt[:, :]))(c, bt, o0),
                         writes=[bb, bqT])
                for c in range(4):
                    bt, bb = proj_fm(*wsel(CZ + (2 + c) * 128), hap512, bh, 512)
                    P.op("act", (lambda c, bt, o0: lambda e: e.activation(out=szA[:, c, o0:o0 + 512], in_=bt[:, :], func=AF.Silu))(c, bt, o0),
                         writes=[bb, bszA])
                if STAGE < 2:
                    continue
                if blk == 0:
                    hh_t, bhh = hTb[1]
                    load_norm_tiles(xh, [TH - 128], hh_t, bhh)
                    conv_u((lambda hh_t: lambda kc: hh_t[:, kc, 0:128])(hh_t), bhh, 128, lambda ch: ubuf[:, ch, 2:130], bubuf)
                    P.op("pool", lambda e: e.tensor_copy(out=ubuf[:, :, 0:2], in_=ubuf[:, :, 128:130]), reads=[bubuf], writes=[bubuf])
                    nres = normB1(xo, [512 + tl * 128 for tl in range(4)])
                conv_u(hap512, bh, 512, lambda ch: ubuf[:, ch, 2:514], bubuf)
                for ch in range(2):
                    bt, bb = proj_fm(*wsel(CZ + ch * 128), hap512, bh, 512)
                    P.op("act", (lambda bt: lambda e: e.activation(out=szl[:, :], in_=bt[:, :], func=AF.Silu))(bt),
                         writes=[bb, bszl])
                    P.op("dve", (lambda ch: lambda e: e.tensor_scalar(out=tcv[:, :], in0=ubuf[:, ch, 0:512],
                                                                      scalar1=convw[:, ch, 0:1], scalar2=None, op0=ALU.mult))(ch),
                         reads=[bubuf, bconvw], writes=[btcv])
                    for j in (1, 2):
                        P.op("dve", (lambda ch, j: lambda e: e.scalar_tensor_tensor(
                            out=tcv[:, :], in0=ubuf[:, ch, j:j + 512], scalar=convw[:, ch, j:j + 1], in1=tcv[:, :],
                            op0=ALU.mult, op1=ALU.add))(ch, j),
                             reads=[bubuf, bconvw, btcv], writes=[btcv])
                    btb, bbb = proj_fm(*wsel(CB + ch * 128), hap512, bh, 512)
                    P.op("dve", (lambda btb: lambda e: e.tensor_tensor(out=tcv[:, :], in0=tcv[:, :], in1=btb[:, :], op=ALU.mult))(btb),
                         reads=[btcv], writes=[bbb, btcv])
                    P.op("dve", (lambda ch, o0: lambda e: e.tensor_tensor(out=mixA[:, ch, o0:o0 + 512], in0=tcv[:, :], in1=szl[:, :], op=ALU.mult))(ch, o0),
                         reads=[btcv, bszl], writes=[bmixA])
                if blk == 3:
                    bt, bb = proj_tm(*wsel(CC), 512, (lambda h_t: lambda kc: h_t[:, kc, 384:512])(h_t), bh)
                    P.op("act", (lambda bt: lambda e: e.activation(out=utok[:, 0:512], in_=bt[:, 0:512], func=AF.Copy))(bt), writes=[bb, butok])
                    P.op("pool", lambda e: e.tensor_tensor(out=utok[:, 0:256], in0=utok[:, 0:256], in1=utok[:, 256:512], op=ALU.mult),
                         reads=[butok], writes=[butok])
                    P.op("pool", lambda e: e.dma_start(out=pconv_d[:, :], in_=utok[126:128, 0:256]), reads=[butok], dma=butok, out=True)
                P.op("pool", lambda e: e.tensor_copy(out=ubuf[:, :, 0:2], in_=ubuf[:, :, 512:514]),
                     reads=[bubuf], writes=[bubuf])
                for cm in range(2):
                    bt, bb = proj_fm(*wsel(CMQ + cm * 128), hap512, bh, 512)
                    P.op("act", (lambda cm, bt: lambda e: e.activation(out=mqT[:, cm, :], in_=bt[:, :], func=AF.Copy))(cm, bt),
                         writes=[bb, bmqT])
                gate = [(szl, bszl), (ccs, bccs)]
                for cm in range(2):
                    btz, bbz = proj_fm(*wsel(CZ + (6 + cm) * 128), hap512, bh, 512)
                    P.op("act", (lambda btz, g_: lambda e: e.activation(out=g_[:, :], in_=btz[:, :], func=AF.Silu))(btz, gate[cm][0]),
                         writes=[bbz, gate[cm][1]])
                for cm in range(2):
                    gz, bgz = gate[cm]
                    for hp in range(2):
                        r0, r1 = hp * 64, hp * 64 + 64
                        ptl = []
                        for mt in range(2):
                            bts, bbs = next_bank()
                            P.op("pe", (lambda bts, mt, r0, r1, cm: lambda e: e.matmul(bts[:, :], lhsT=mkT[r0:r1, cm, mt * 128:(mt + 1) * 128],
                                                                          rhs=mqT[r0:r1, cm, :], start=True, stop=True))(bts, mt, r0, r1, cm),
                                 reads=[bmkT, bmqT], writes=[bbs])
                            pt, bpt = pmT[cnt["pm"] % 2]; cnt["pm"] += 1
                            P.op("act", (lambda pt, bts: lambda e: e.activation(out=pt[:, :], in_=bts[:, :], func=AF.Exp, scale=0.125))(pt, bts),
                                 writes=[bbs, bpt])
                            ptl.append((pt, bpt))
                        bta, bba = next_bank()
                        btd, bbd = next_bank()
                        for mt in range(2):
                            pt, bpt = ptl[mt]
                            P.op("pe", (lambda pt, mt, bta, cm: lambda e: e.matmul(bta[:, :], lhsT=mvb[:, mt, cm * 128:(cm + 1) * 128], rhs=pt[:, :],
                                                                         start=(mt == 0), stop=(mt == 1)))(pt, mt, bta, cm),
                                 reads=[bmvb, bpt], writes=[bba])
                        for mt in range(2):
                            pt, bpt = ptl[mt]
                            P.op("pe", (lambda pt, mt, btd: lambda e: e.matmul(btd[:, :], lhsT=ones_b[:, :], rhs=pt[:, :],
                                                                         start=(mt == 0), stop=(mt == 1)))(pt, mt, btd),
                                 reads=[bones, bpt], writes=[bbd])
                        P.op("act", (lambda btd, r0, r1: lambda e: e.activation(out=rdn[r0:r1, :], in_=btd[r0:r1, :], func=AF.Ln))(btd, r0, r1),
                             writes=[bbd, brdn])
                        P.op("act", (lambda r0, r1: lambda e: e.activation(out=rdn[r0:r1, :], in_=rdn[r0:r1, :], func=AF.Exp, scale=-1.0))(r0, r1),
                             writes=[brdn])
                        P.op("dve", (lambda bta, r0, r1: lambda e: e.tensor_tensor(out=rdn[r0:r1, :], in0=rdn[r0:r1, :], in1=bta[r0:r1, :], op=ALU.mult))(bta, r0, r1),
                             reads=[brdn], writes=[bba, brdn])
                        P.op("dve", (lambda r0, r1, cm, o0, gz: lambda e: e.tensor_tensor(out=mixA[r0:r1, 2 + cm, o0:o0 + 512], in0=rdn[r0:r1, :], in1=gz[r0:r1, :], op=ALU.mult))(r0, r1, cm, o0, gz),
                             reads=[brdn, bgz], writes=[bmixA])
                if nres is not None:
                    normB2(nres, hTb[(blk + 1) % 2][0], hTb[(blk + 1) % 2][1])

        pAB.close()
        pr.close()
        collect(ph, [bWmem, bWrA])
        collect(pAB)
        bar = sorted(freed)

        prS = contextlib.ExitStack()
        es.enter_context(prS)

        def right_buf(name, shape, dt):
            t = prS.enter_context(nc.sbuf_tensor("s_" + name, list(shape), dt, side="right"))
            b = Buf(name)
            b.readers = list(bar)
            return t, b

        Qbd, bQbd = right_buf("Qbd", [128, NB, 4, 32], BF16)
        Qbdm, bQbdm = right_buf("Qbdm", [128, NB, 2, 16], BF16)
        Kt0 = right_buf("Kt0r", [128, 8, 512], BF16)
        Vt0 = right_buf("Vt0r", [128, 8, 512], BF16)
        mkb0 = right_buf("mkb0r", [128, 2, 256], BF16)
        mvs0 = right_buf("mvs0r", [128, 2, 256], BF16)
        def prefetch_batch0():
            P.op("pool", lambda e: e.memset(Qbd[:], 0.0), writes=[bQbd])
            P.op("pool", lambda e: e.memset(Qbdm[:], 0.0), writes=[bQbdm])
            for c in range(4):
                for half in range(2):
                    h = 2 * c + half
                    r0, r1 = half * 64, half * 64 + 64
                    P.op("dve", (lambda c, h, r0, r1: lambda e: e.tensor_copy(
                        out=Qbd[r0:r1, :, c, h * 4:h * 4 + 4], in_=qS[r0:r1, c, :].rearrange("p (b j) -> p b j", j=4)))(c, h, r0, r1),
                         reads=[bqS], writes=[bQbd])
            for cm in range(2):
                for half in range(2):
                    h = 2 * cm + half
                    r0, r1 = half * 64, half * 64 + 64
                    P.op("dve", (lambda cm, h, r0, r1: lambda e: e.tensor_copy(
                        out=Qbdm[r0:r1, :, cm, h * 4:h * 4 + 4], in_=mqS[r0:r1, cm, :].rearrange("p (b j) -> p b j", j=4)))(cm, h, r0, r1),
                         reads=[bmqS], writes=[bQbdm])

            for (tt_, bb__) in (Kt0, Vt0):
                P.op("pool", (lambda tt_: lambda e: e.memset(tt_[64:128, 0:4, :], 0.0))(tt_), writes=[bb__])
            for (dstb, srcd) in ((Kt0, cwk), (Vt0, cwv)):
                s16 = bass.AP(tensor=srcd.tensor, offset=0, ap=[[16 * 512, 96], [1, 2048]])
                s4 = bass.AP(tensor=srcd.tensor, offset=1536 * 512, ap=[[2048, 128], [1, 2048]])
                P.op("pool", (lambda dst, s16: lambda e: e.dma_start(out=dst[0:96, 0:4, :].rearrange("p a n -> p (a n)"), in_=s16))(dstb[0], s16),
                     writes=[dstb[1]], dma=dstb[1])
                P.op("pool", (lambda dst, s4: lambda e: e.dma_start(out=dst[:, 4:8, :].rearrange("p a n -> p (a n)"), in_=s4))(dstb[0], s4),
                     writes=[dstb[1]], dma=dstb[1])
            P.op("pool", lambda e: e.dma_start(out=mkb0[0][:], in_=cmk[0].rearrange("(t p) n -> p t n", p=128)), writes=[mkb0[1]], dma=mkb0[1])
            P.op("pool", lambda e: e.dma_start(out=mvs0[0][:], in_=cmv[0].rearrange("(t p) n -> p t n", p=128)), writes=[mvs0[1]], dma=mvs0[1])


        if STAGE >= 2:
            pc = contextlib.ExitStack()
            with pc:
                accs = [phase_buf("acc%d" % i, [128, 2, T], F32, pc) for i in range(2)]
                btabs = [phase_buf("btab%d" % i, [128, 3 * 2 * 512], BF16, pc) for i in range(2)]
                vds = [phase_buf("vd%d" % i, [128, 32, 128], BF16, pc) for i in range(2)]
                pts = [phase_buf("ptT%d" % i, [128, 512], BF16, pc) for i in range(4)]

                items = []
                groups = []
                for pair in range(4):
                    for ci, dil in enumerate(DILS):
                        g = len(groups)
                        groups.append((pair, ci, dil))
                        nq = 16 // dil
                        qtiles = [(r, qt) for r in range(dil) for qt in range(nq)]
                        for u0 in range(0, 16, 2):
                            for hp in range(2):
                                items.append(dict(pair=pair, ci=ci, dil=dil, g=g, tl=qtiles[u0:u0 + 2], hp=hp,
                                                  first=(u0 == 0 and hp == 0), last_of_pair=(ci == 2 and u0 == 14 and hp == 1)))

                def load_group(g):
                    pair, ci, dil = groups[g]
                    vd, bvd = vds[g % 2]
                    ntile = 32 // dil
                    for r in range(dil):
                        j0 = 15 if dil == 1 else 0
                        src = bass.AP(tensor=vs_d.tensor, offset=(pair * (TH + T) + r + j0 * 128 * dil) * 128,
                                      ap=[[dil * 128, 128], [128 * dil * 128, ntile - j0], [1, 128]])
                        P.op("sp", (lambda vd, r, ntile, j0, src: lambda e: e.dma_start(
                            out=vd[:, r * ntile + j0:(r + 1) * ntile, :], in_=src))(vd, r, ntile, j0, src),
                             reads=[bvs], writes=[bvd], dma=bvd)

                def load_btab(pair):
                    bt_t, bbt_ = btabs[pair % 2]
                    P.op("pool", (lambda bt_t, pair: lambda e: e.dma_start(out=bt_t[:], in_=btab_d[pair]))(bt_t, pair),
                         writes=[bbt_], dma=bbt_)

                def front_mms(k):
                    it = items[k]
                    pair, ci, dil, hp, tl = it["pair"], it["ci"], it["dil"], it["hp"], it["tl"]
                    r0, r1 = hp * 64, hp * 64 + 64
                    starts = [qt * 128 * dil + r for (r, qt) in tl]
                    bX, bbX = next_bank()
                    bt_t, bbt_ = btabs[pair % 2]
                    tb0 = (ci * 2 + hp) * 512
                    mms = [(lambda bX, bbX, bt_t, bbt_, tb0: lambda: P.op("pe", lambda e: e.matmul(
                        bX[:, :], lhsT=ident_b[:, :], rhs=bt_t[:, tb0:tb0 + 512], start=True, stop=False),
                        reads=[bidb, bbt_], writes=[bbX]))(bX, bbX, bt_t, bbt_, tb0)]
                    nmm = 0
                    for ui, (r, qt) in enumerate(tl):
                        qs = starts[ui]
                        q_ap = qT[r0:r1, pair, qs:qs + 127 * dil + 1:dil]
                        for ab in range(2):
                            ks = TH + qs - (1 - ab) * 128 * dil
                            k_ap = kT[r0:r1, pair, ks:ks + 127 * dil + 1:dil]
                            col = ui * 256 + ab * 128
                            nmm += 1
                            mms.append((lambda bX, bbX, col, k_ap, q_ap, last: lambda: P.op("pe", lambda e: e.matmul(
                                bX[:, col:col + 128], lhsT=k_ap, rhs=q_ap, start=False, stop=last),
                                reads=[bkT, bqT], writes=[bbX]))(bX, bbX, col, k_ap, q_ap, nmm == 4))
                    return mms, (bX, bbX)

                def front_ew(k, ctx):
                    bX, bbX = ctx
                    ptt, bpt = pts[k % len(pts)]
                    P.op("act", (lambda bX, ptt: lambda e: e.activation(out=ptt[:, :], in_=bX[:, :], func=AF.Exp, scale=0.125))(bX, ptt),
                         writes=[bbX, bpt])

                def front2(k):
                    m0, c0_ = front_mms(k)
                    m1, c1_ = front_mms(k + 1)
                    for f0, f1 in zip(m0, m1):
                        f0()
                        f1()
                    front_ew(k, c0_)
                    front_ew(k + 1, c1_)

                def back(k):
                    it = items[k]
                    pair, ci, dil, hp, tl, g = it["pair"], it["ci"], it["dil"], it["hp"], it["tl"], it["g"]
                    r0, r1 = hp * 64, hp * 64 + 64
                    ntile = 32 // dil
                    vd, bvd = vds[g % 2]
                    ptt, bpt = pts[k % len(pts)]
                    acc, bacc = accs[pair % 2]
                    starts = [qt * 128 * dil + r for (r, qt) in tl]
                    delta = starts[1] - starts[0]
                    bY, bbY = next_bank()
                    for ui, (r, qt) in enumerate(tl):
                        for ab in range(2):
                            J = 16 // dil + qt - (1 - ab)
                            col = ui * 256 + ab * 128
                            P.op("pe", (lambda bY, ui, ab, vd, tix, ptt, col: lambda e: e.matmul(
                                bY[:, ui * 128:(ui + 1) * 128], lhsT=vd[:, tix, :], rhs=ptt[:, col:col + 128],
                                start=(ab == 0), stop=(ab == 1)))(bY, ui, ab, vd, r * ntile + J, ptt, col),
                                 reads=[bvd, bpt], writes=[bbY])
                    haloA = [(16 // dil + qt - 1) < 16 // dil for (r, qt) in tl]
                    if haloA[0] == haloA[1]:
                        pv4 = ptt[:, :].rearrange("p (u ab m) -> p u ab m", u=2, ab=2)
                        for ab in range(2):
                            on = hones_b if (ab == 0 and haloA[0]) else ones_b
                            P.op("pe", (lambda bY, ab, on, pv4: lambda e: e.matmul(
                                bY[:, 256:512], lhsT=on[:, :], rhs=pv4[:, :, ab, :], start=(ab == 0), stop=(ab == 1)))(bY, ab, on, pv4),
                                 reads=[bhob, bones, bpt], writes=[bbY])
                    else:
                        for ui, (r, qt) in enumerate(tl):
                            for ab in range(2):
                                J = 16 // dil + qt - (1 - ab)
                                halo = J < 16 // dil
                                col = ui * 256 + ab * 128
                                on = hones_b if halo else ones_b
                                P.op("pe", (lambda bY, ui, ab, on, ptt, col: lambda e: e.matmul(
                                    bY[:, 256 + ui * 128:256 + (ui + 1) * 128], lhsT=on[:, :], rhs=ptt[:, col:col + 128],
                                    start=(ab == 0), stop=(ab == 1)))(bY, ui, ab, on, ptt, col),
                                     reads=[bhob, bones, bpt], writes=[bbY])
                    dst = sub_ap(acc[r0:r1, 0, starts[0]:starts[0] + 1], [[T, 2], [delta, 2], [dil, 128]])
                    srcp = bY[r0:r1, :].rearrange("p (w u m) -> p w u m", w=2, m=128)
                    if ci == 0 and hp == 0:
                        P.op("act", (lambda dst, srcp: lambda e: e.activation(out=dst, in_=srcp, func=AF.Copy))(dst, srcp),
                             writes=[bbY, bacc])
                    elif ci == 0:
                        P.op("dve", (lambda dst, srcp: lambda e: e.tensor_copy(out=dst, in_=srcp))(dst, srcp),
                             writes=[bbY, bacc])
                    else:
                        P.op("dve", (lambda dst, srcp: lambda e: e.tensor_tensor(out=dst, in0=srcp, in1=dst, op=ALU.add))(dst, srcp),
                             writes=[bbY, bacc])
                    if it["last_of_pair"]:
                        for q4 in range(4):
                            cs = slice(q4 * 512, (q4 + 1) * 512)
                            pending.append((lambda acc, bacc, cs, pair: lambda: (
                                P.op("act", lambda e: e.activation(out=acc[:, 1, cs], in_=acc[:, 1, cs], func=AF.Ln), writes=[bacc]),
                                P.op("act", lambda e: e.activation(out=acc[:, 1, cs], in_=acc[:, 1, cs], func=AF.Exp, scale=-1.0), writes=[bacc]),
                                P.op("dve" if flush_on_dve[0] else "pool",
                                     lambda e: e.tensor_tensor(out=acc[:, 0, cs], in0=acc[:, 0, cs], in1=acc[:, 1, cs], op=ALU.mult),
                                     writes=[bacc]),
                                P.op("dve" if flush_on_dve[0] else "pool",
                                     lambda e: e.tensor_tensor(out=mixB[:, pair, cs], in0=acc[:, 0, cs], in1=szA[:, pair, cs], op=ALU.mult),
                                     reads=[bacc, bszA], writes=[bmixB])))(acc, bacc, cs, pair))

                NI = len(items)
                assert NI % 2 == 0
                pending = []
                flush_on_dve = [False]
                load_btab(0)
                load_btab(1)
                load_group(0)
                load_group(1)
                if STAGE >= 3:
                    prefetch_batch0()
                front2(0)
                for k in range(0, NI, 2):
                    it = items[k]
                    if k + 2 < NI:
                        nx = items[k + 2]
                        front2(k + 2)
                        if nx["first"] and nx["ci"] == 0 and 0 < nx["pair"] < 3:
                            load_btab(nx["pair"] + 1)
                    back(k)
                    if pending and not items[k + 1]["last_of_pair"]:
                        pending.pop(0)()
                    back(k + 1)
                    if k + 2 == NI or items[k + 2]["g"] != it["g"]:
                        if it["g"] + 2 < len(groups):
                            load_group(it["g"] + 2)
                flush_on_dve[0] = True
                while pending:
                    pending.pop(0)()

            collect(pc)

        pk.close()
        collect(pk)
        bar = sorted(freed)
        if STAGE >= 3:
            psx = contextlib.ExitStack()
            with psx:
                wmain, bwmain = phase_buf("wmain", [32, 1024], F32, psx)
                wnew, bwnew = phase_buf("wnew", [32, NB * 64], F32, psx)
                bmask, bbmask = phase_buf("bmask", [32, 512], F32, psx)
                bmaskm, bbmaskm = phase_buf("bmaskm", [16, 256], F32, psx)
                par, bpar = phase_buf("par", [32, 2], F32, psx)
                for tt, bb_, dd in ((wmain, bwmain, wmain_d), (wnew, bwnew, wnew_d), (bmask, bbmask, bmask_d),
                                    (bmaskm, bbmaskm, bmaskm_d), (par, bpar, par_d)):
                    P.op("sp", (lambda tt, dd: lambda e: e.dma_start(out=tt[:], in_=dd))(tt, dd), writes=[bb_], dma=bb_)
                Oall, bOall = phase_buf("Oall", [32, NB, 64], F32, psx)
                Omall, bOmall = phase_buf("Omall", [16, NB, 64], F32, psx)
                Opad, bOpad = phase_buf("Opad", [32, NB, 128], F32, psx)
                Ompad, bOmpad = phase_buf("Ompad", [16, NB, 128], F32, psx)
                L1, bL1 = phase_buf("L1", [32, NB], F32, psx)
                L2, bL2 = phase_buf("L2", [32, NB], F32, psx)
                Lm, bLm = phase_buf("Lm", [16, NB], F32, psx)
                Kt = [Kt0, phase_buf("Kt1", [128, 8, 512], BF16, psx)]
                Vt = [Vt0, phase_buf("Vt1", [128, 8, 512], BF16, psx)]
                KTs = [phase_buf("KTs%d" % i, [128, 4, 1024], BF16, psx) for i in range(2)]
                mkb = [mkb0, phase_buf("mkb1", [128, 2, 256], BF16, psx)]
                mvs = [mvs0, phase_buf("mvs1", [128, 2, 256], BF16, psx)]
                mkTs = [phase_buf("mkTs%d" % i, [128, 2, 256], BF16, psx) for i in range(2)]
                Eb = [phase_buf("Eb%d" % i, [32, 1088], F32, psx) for i in range(2)]
                Pb = [phase_buf("Pb%d" % i, [32, 1088], BF16, psx) for i in range(2)]
                Pmb = [phase_buf("Pmb%d" % i, [16, 256], BF16, psx) for i in range(2)]
                PTb = [phase_buf("PTb%d" % i, [128, 320], BF16, psx) for i in range(2)]
                tmpb = [phase_buf("tmpb%d" % i, [32, 512], F32, psx) for i in range(2)]
                tmpm = [phase_buf("tmpm%d" % i, [16, 256], F32, psx) for i in range(2)]

                def load_k(b):
                    kt, bkt = Kt[b % 2]
                    mk_, bmk_ = mkb[b % 2]
                    s16 = bass.AP(tensor=cwk.tensor, offset=b * 2048 * 512, ap=[[16 * 512, 96], [1, 2048]])
                    s4 = bass.AP(tensor=cwk.tensor, offset=(b * 2048 + 1536) * 512, ap=[[2048, 128], [1, 2048]])
                    P.op("pool", (lambda dst, s16: lambda e: e.dma_start(out=dst[0:96, 0:4, :].rearrange("p a n -> p (a n)"), in_=s16))(kt, s16),
                         writes=[bkt], dma=bkt)
                    P.op("pool", (lambda dst, s4: lambda e: e.dma_start(out=dst[:, 4:8, :].rearrange("p a n -> p (a n)"), in_=s4))(kt, s4),
                         writes=[bkt], dma=bkt)
                    P.op("pool", (lambda mk_, b: lambda e: e.dma_start(out=mk_[:], in_=cmk[b].rearrange("(t p) n -> p t n", p=128)))(mk_, b),
                         writes=[bmk_], dma=bmk_)

                def load_v(b):
                    vt, bvt = Vt[b % 2]
                    mv_, bmv_ = mvs[b % 2]
                    s16 = bass.AP(tensor=cwv.tensor, offset=b * 2048 * 512, ap=[[16 * 512, 96], [1, 2048]])
                    s4 = bass.AP(tensor=cwv.tensor, offset=(b * 2048 + 1536) * 512, ap=[[2048, 128], [1, 2048]])
                    P.op("pool", (lambda dst, s16: lambda e: e.dma_start(out=dst[0:96, 0:4, :].rearrange("p a n -> p (a n)"), in_=s16))(vt, s16),
                         writes=[bvt], dma=bvt)
                    P.op("pool", (lambda dst, s4: lambda e: e.dma_start(out=dst[:, 4:8, :].rearrange("p a n -> p (a n)"), in_=s4))(vt, s4),
                         writes=[bvt], dma=bvt)
                    P.op("pool", (lambda mv_, b: lambda e: e.dma_start(out=mv_[:], in_=cmv[b].rearrange("(t p) n -> p t n", p=128)))(mv_, b),
                         writes=[bmv_], dma=bmv_)

                xr = [phase_buf("xr%d" % i, [128, D], F32, psx) for i in range(2)]
                rs = [phase_buf("rs%d" % i, [128, D], F32, psx) for i in range(2)]
                yo = [phase_buf("yo%d" % i, [128, D], F32, psx) for i in range(2)]
                stat2, bstat2 = phase_buf("stat2", [128, 8], F32, psx)
                Wo, bWo = phase_buf("Wo", [128, 8, D], BF16, psx)
                gfin, bgfin = phase_buf("gfin", [128, D], F32, psx)

                def mix_prompt(c, tsl):
                    if c < 2:
                        return mixA[:, c, tsl]
                    if c < 6:
                        return mixB[:, c - 2, tsl]
                    return mixA[:, c - 4, tsl]

                def out_tile(src_rows, dst_rows, mix_fn, bmx, tsl, n, k):
                    x_t, bx = xr[k % 2]
                    r_t, br = rs[k % 2]
                    y_t, by = yo[k % 2]
                    P.op("sp", lambda e: e.dma_start(out=x_t[0:n, :], in_=src_rows), writes=[bx], dma=bx)
                    yield
                    hb = []
                    for half in range(2):
                        bt, bb = next_bank("aux")
                        for c in range(8):
                            P.op("pe", (lambda c, bt, half: lambda e: e.matmul(bt[0:n, :], lhsT=mix_fn(c, tsl), rhs=Wo[:, c, half * 512:(half + 1) * 512],
                                                                             start=(c == 0), stop=(c == 7)))(c, bt, half),
                                 reads=list(bmx) + [bWo], writes=[bb])
                        P.op("dve", (lambda bt, half: lambda e: e.tensor_tensor(out=r_t[0:n, half * 512:(half + 1) * 512], in0=bt[0:n, :],
                                                                               in1=x_t[0:n, half * 512:(half + 1) * 512], op=ALU.add))(bt, half),
                             reads=[bx], writes=[bb, br])
                        yield
                    col = k % 8
                    P.op("act", lambda e: e.activation(out=y_t[0:n, :], in_=r_t[0:n, :], func=AF.Square, accum_out=stat2[0:n, col:col + 1]),
                         reads=[br], writes=[bstat2, by])
                    P.op("act", lambda e: e.activation(out=stat2[0:n, col:col + 1], in_=stat2[0:n, col:col + 1], func=AF.Ln,
                                                       scale=1.0 / D, bias=epsb[0:n, 0:1]), reads=[bstat2, bepsb], writes=[bstat2])
                    P.op("act", lambda e: e.activation(out=stat2[0:n, col:col + 1], in_=stat2[0:n, col:col + 1], func=AF.Exp, scale=-0.5),
                         reads=[bstat2], writes=[bstat2])
                    yield
                    P.op("dve", lambda e: e.scalar_tensor_tensor(out=y_t[0:n, :], in0=r_t[0:n, :], scalar=stat2[0:n, col:col + 1], in1=gfin[0:n, :],
                                                                 op0=ALU.mult, op1=ALU.mult), reads=[br, bstat2, bgfin], writes=[by])
                    yield
                    P.op("sp", lambda e: e.dma_start(out=dst_rows, in_=y_t[0:n, :]), reads=[by], dma=by, out=True)
                    yield


                bank_pool["main"] = list(range(6))

                def stageA(b):
                    kt, bkt = Kt[b % 2]; vt, bvt = Vt[b % 2]
                    mk_, bmk_ = mkb[b % 2]; mv_, bmv_ = mvs[b % 2]
                    KT_, bKT_ = KTs[b % 2]; mkT_, bmkT_ = mkTs[b % 2]
                    E_, bE_ = Eb[b % 2]; P_, bP_ = Pb[b % 2]; Pm_, bPm_ = Pmb[b % 2]; PT_, bPT_ = PTb[b % 2]
                    tp_, btp_ = tmpb[b % 2]; tm_, btm_ = tmpm[b % 2]
                    for c in range(4):
                        bt, bb = next_bank()
                        v = bf(bt)
                        for t8 in range(8):
                            P.op("pe", (lambda v, t8, kt, c: lambda e: e.transpose(v[:, t8 * 128:(t8 + 1) * 128], kt[:, t8, c * 128:(c + 1) * 128],
                                                                                 ident_b[:, :]))(v, t8, kt, c),
                                 reads=[bkt, bidb], writes=[bb])
                        if c % 2 == 0:
                            P.op("act", (lambda v, KT_, c: lambda e: e.activation(out=KT_[:, c, :], in_=v[:, :], func=AF.Copy))(v, KT_, c),
                                 writes=[bb, bKT_])
                        else:
                            P.op("dve", (lambda v, KT_, c: lambda e: e.tensor_copy(out=KT_[:, c, :], in_=v[:, :]))(v, KT_, c),
                                 writes=[bb, bKT_])
                        yield
                    bt, bb = next_bank()
                    v = bf(bt)
                    for cm in range(2):
                        for mt in range(2):
                            P.op("pe", (lambda v, cm, mt, mk_: lambda e: e.transpose(v[:, (cm * 2 + mt) * 128:(cm * 2 + mt + 1) * 128],
                                                                                  mk_[:, mt, cm * 128:(cm + 1) * 128], ident_b[:, :]))(v, cm, mt, mk_),
                                 reads=[bmk_, bidb], writes=[bb])
                    P.op("act", (lambda v, mkT_: lambda e: e.activation(out=mkT_[:, :, :].rearrange("p c m -> p (c m)"), in_=v[:, 0:512], func=AF.Copy))(v, mkT_),
                         writes=[bb, bmkT_])
                    yield
                    bS0, bbS0 = next_bank(); bS1, bbS1 = next_bank(); bS2, bbS2 = next_bank(); bSm, bbSm = next_bank()
                    for c in range(4):
                        q_ap = Qbd[:, b, c, :]
                        P.op("pe", (lambda bS0, q_ap, KT_, c: lambda e: e.matmul(bS0[0:32, :], lhsT=q_ap, rhs=KT_[:, c, 0:512], start=(c == 0), stop=(c == 3)))(bS0, q_ap, KT_, c),
                             reads=[bQbd, bKT_], writes=[bbS0])
                        P.op("pe", (lambda bS1, q_ap, KT_, c: lambda e: e.matmul(bS1[0:32, :], lhsT=q_ap, rhs=KT_[:, c, 512:1024], start=(c == 0), stop=(c == 3)))(bS1, q_ap, KT_, c),
                             reads=[bQbd, bKT_], writes=[bbS1])
                        P.op("pe", (lambda bS2, q_ap, c: lambda e: e.matmul(bS2[0:32, 0:NS], lhsT=q_ap, rhs=kS[:, c, :], start=(c == 0), stop=(c == 3)))(bS2, q_ap, c),
                             reads=[bQbd, bkS], writes=[bbS2])
                    for cm in range(2):
                        P.op("pe", (lambda bSm, cm, mkT_, b: lambda e: e.matmul(bSm[0:16, 0:256], lhsT=Qbdm[:, b, cm, :], rhs=mkT_[:, cm, :], start=(cm == 0), stop=(cm == 1)))(bSm, cm, mkT_, b),
                             reads=[bQbdm, bmkT_], writes=[bbSm])
                    yield
                    P.op("act", (lambda E_, bS0: lambda e: e.activation(out=E_[:, 0:512], in_=bS0[0:32, :], func=AF.Exp, scale=0.125))(E_, bS0), writes=[bbS0, bE_])
                    P.op("act", (lambda E_, bS1: lambda e: e.activation(out=E_[:, 512:1024], in_=bS1[0:32, :], func=AF.Exp, scale=0.125))(E_, bS1), writes=[bbS1, bE_])
                    P.op("act", (lambda E_, bS2: lambda e: e.activation(out=E_[:, 1024:1088], in_=bS2[0:32, 0:NS], func=AF.Exp, scale=0.125))(E_, bS2), writes=[bbS2, bE_])
                    P.op("act", (lambda Pm_, bSm, b: lambda e: e.activation(out=Pm_[:, :], in_=bSm[0:16, 0:256], func=AF.Exp, scale=0.125, accum_out=Lm[:, b:b + 1]))(Pm_, bSm, b),
                         writes=[bbSm, bPm_, bLm])
                    yield
                    P.op("dve", (lambda P_, E_, b: lambda e: e.scalar_tensor_tensor(out=P_[:, 0:1024], in0=E_[:, 0:1024], scalar=1.0, in1=wmain[:, :],
                                                                                op0=ALU.mult, op1=ALU.mult, accum_out=L1[:, b:b + 1]))(P_, E_, b),
                         reads=[bE_, bwmain], writes=[bP_, bL1])
                    P.op("dve", (lambda P_, E_, b: lambda e: e.scalar_tensor_tensor(out=P_[:, 1024:1088], in0=E_[:, 1024:1088], scalar=1.0, in1=wnew[:, b * 64:(b + 1) * 64],
                                                                                op0=ALU.mult, op1=ALU.mult, accum_out=L2[:, b:b + 1]))(P_, E_, b),
                         reads=[bE_, bwnew], writes=[bP_, bL2])

                def stageB(b):
                    kt, bkt = Kt[b % 2]; vt, bvt = Vt[b % 2]
                    mk_, bmk_ = mkb[b % 2]; mv_, bmv_ = mvs[b % 2]
                    KT_, bKT_ = KTs[b % 2]; mkT_, bmkT_ = mkTs[b % 2]
                    E_, bE_ = Eb[b % 2]; P_, bP_ = Pb[b % 2]; Pm_, bPm_ = Pmb[b % 2]; PT_, bPT_ = PTb[b % 2]
                    tp_, btp_ = tmpb[b % 2]; tm_, btm_ = tmpm[b % 2]
                    bt, bb = next_bank()
                    v = bf(bt)
                    for t8 in range(8):
                        P.op("pe", (lambda v, t8, P_: lambda e: e.transpose(v[:, t8 * 32:(t8 + 1) * 32], P_[0:32, t8 * 128:(t8 + 1) * 128], ident_b[0:32, 0:32]))(v, t8, P_),
                             reads=[bP_, bidb], writes=[bb])
                    P.op("pe", (lambda v, P_: lambda e: e.transpose(v[0:64, 256:288], P_[0:32, 1024:1088], ident_b[0:32, 0:32]))(v, P_),
                         reads=[bP_, bidb], writes=[bb])
                    for mt in range(2):
                        P.op("pe", (lambda v, mt, Pm_: lambda e: e.transpose(v[:, 288 + mt * 16:288 + (mt + 1) * 16], Pm_[0:16, mt * 128:(mt + 1) * 128], ident_b[0:16, 0:16]))(v, mt, Pm_),
                             reads=[bPm_, bidb], writes=[bb])
                    yield
                    P.op("dve", (lambda v, PT_: lambda e: e.tensor_copy(out=PT_[:, :], in_=v[:, 0:320]))(v, PT_), writes=[bb, bPT_])
                    yield
                    bO, bbO = next_bank(); bOm, bbOm = next_bank()
                    for t8 in range(8):
                        P.op("pe", (lambda bO, t8, PT_, vt: lambda e: e.matmul(bO[0:32, :], lhsT=PT_[:, t8 * 32:(t8 + 1) * 32], rhs=vt[:, t8, :], start=(t8 == 0), stop=False))(bO, t8, PT_, vt),
                             reads=[bPT_, bvt], writes=[bbO])
                    P.op("pe", (lambda bO, PT_: lambda e: e.matmul(bO[0:32, :], lhsT=PT_[0:64, 256:288], rhs=vSb[0:64, :], start=False, stop=True))(bO, PT_),
                         reads=[bPT_, bvSb], writes=[bbO])
                    for mt in range(2):
                        P.op("pe", (lambda bOm, mt, PT_, mv_: lambda e: e.matmul(bOm[0:16, 0:256], lhsT=PT_[:, 288 + mt * 16:288 + (mt + 1) * 16], rhs=mv_[:, mt, :], start=(mt == 0), stop=(mt == 1)))(bOm, mt, PT_, mv_),
                             reads=[bPT_, bmv_], writes=[bbOm])
                    yield
                    P.op("dve", (lambda tp_, bO: lambda e: e.tensor_tensor(out=tp_[:, :], in0=bO[0:32, :], in1=bmask[:, :], op=ALU.mult))(tp_, bO),
                         reads=[bbmask], writes=[bbO, btp_])
                    P.op("dve", (lambda tp_, b: lambda e: e.tensor_reduce(out=Oall[:, b, :], in_=tp_[:, :].rearrange("p (h d) -> p d h", d=64), axis=AX.X, op=ALU.add))(tp_, b),
                         reads=[btp_], writes=[bOall])
                    yield
                    P.op("dve", (lambda tm_, bOm: lambda e: e.tensor_tensor(out=tm_[:, :], in0=bOm[0:16, 0:256], in1=bmaskm[:, :], op=ALU.mult))(tm_, bOm),
                         reads=[bbmaskm], writes=[bbOm, btm_])
                    P.op("dve", (lambda tm_, b: lambda e: e.tensor_reduce(out=Omall[:, b, :], in_=tm_[:, :].rearrange("p (h d) -> p d h", d=64), axis=AX.X, op=ALU.add))(tm_, b),
                         reads=[btm_], writes=[bOmall])

                def interleave(gens):
                    gens = list(gens)
                    while gens:
                        for g in list(gens):
                            try:
                                next(g)
                            except StopIteration:
                                gens.remove(g)

                for (tt_, bb__) in (Kt[1], Vt[1]):
                    P.op("pool", (lambda tt_: lambda e: e.memset(tt_[64:128, 0:4, :], 0.0))(tt_), writes=[bb__])
                for hh in range(2):
                    P.op("pool", (lambda hh: lambda e: e.dma_start(
                        out=Wo[:, hh * 4:(hh + 1) * 4, :],
                        in_=w_out[hh * 512:(hh + 1) * 512, :].rearrange("(k p) n -> p k n", p=128)))(hh),
                         writes=[bWo], dma=bWo)
                P.op("sp", lambda e: e.dma_start(out=gfin[:], in_=gfin_d), writes=[bgfin], dma=bgfin)
                load_k(1); load_v(1)
                interleave([stageA(0)])
                for b in range(NB):
                    if b + 2 < NB:
                        load_k(b + 2)
                    gens = [stageB(b), out_tile(xo[b * 128:(b + 1) * 128, :], y_d[b * 128:(b + 1) * 128, :], mix_prompt,
                                                [bmixA, bmixB], slice(b * 128, (b + 1) * 128), 128, b)]
                    if b + 1 < NB:
                        gens.insert(0, stageA(b + 1))
                    interleave(gens)
                    if b + 2 < NB:
                        load_v(b + 2)
                P.op("dve", lambda e: e.tensor_tensor(out=L1[:, :], in0=L1[:, :], in1=L2[:, :], op=ALU.add), reads=[bL2], writes=[bL1])
                P.op("dve", lambda e: e.reciprocal(out=L1[:, :], in_=L1[:, :]), writes=[bL1])
                P.op("dve", lambda e: e.reciprocal(out=Lm[:, :], in_=Lm[:, :]), writes=[bLm])
                L1b = sub_ap(L1[:, 0:1], [[1, NB], [0, 64]])
                Lmb = sub_ap(Lm[:, 0:1], [[1, NB], [0, 64]])
                for k in range(2):
                    P.op("dve", (lambda k: lambda e: e.scalar_tensor_tensor(out=Opad[:, :, k * 64:(k + 1) * 64], in0=Oall[:, :, :], scalar=par[:, k:k + 1],
                                                                            in1=L1b, op0=ALU.mult, op1=ALU.mult))(k),
                         reads=[bOall, bL1, bpar], writes=[bOpad])
                    P.op("dve", (lambda k: lambda e: e.scalar_tensor_tensor(out=Ompad[:, :, k * 64:(k + 1) * 64], in0=Omall[:, :, :], scalar=par[0:16, k:k + 1],
                                                                            in1=Lmb, op0=ALU.mult, op1=ALU.mult))(k),
                         reads=[bOmall, bLm, bpar], writes=[bOmpad])
                bT, bbT = next_bank(); bTm, bbTm = next_bank()
                for b in range(NB):
                    P.op("pe", (lambda b: lambda e: e.transpose(bT[:, b * 32:(b + 1) * 32], Opad[0:32, b, :], ident_f[0:32, 0:32]))(b),
                         reads=[bOpad, bidf], writes=[bbT])
                    P.op("pe", (lambda b: lambda e: e.transpose(bTm[:, b * 16:(b + 1) * 16], Ompad[0:16, b, :], ident_f[0:16, 0:16]))(b),
                         reads=[bOmpad, bidf], writes=[bbTm])
                for c in range(4):
                    for half in range(2):
                        h = 2 * c + half
                        r0, r1 = half * 64, half * 64 + 64
                        P.op("dve", (lambda c, h, r0, r1: lambda e: e.tensor_tensor(
                            out=mixS[r0:r1, 2 + c, :].rearrange("p (b j) -> p b j", j=4),
                            in0=bT[r0:r1, :].rearrange("p (b x) -> p b x", x=32)[:, :, h * 4:h * 4 + 4],
                            in1=szS[r0:r1, 2 + c, :].rearrange("p (b j) -> p b j", j=4), op=ALU.mult))(c, h, r0, r1),
                             reads=[bszS], writes=[bbT, bmixS])
                for cm in range(2):
                    for half in range(2):
                        h = 2 * cm + half
                        r0, r1 = half * 64, half * 64 + 64
                        P.op("dve", (lambda cm, h, r0, r1: lambda e: e.tensor_tensor(
                            out=mixS[r0:r1, 6 + cm, :].rearrange("p (b j) -> p b j", j=4),
                            in0=bTm[r0:r1, 0:256].rearrange("p (b x) -> p b x", x=16)[:, :, h * 4:h * 4 + 4],
                            in1=szS[r0:r1, 6 + cm, :].rearrange("p (b j) -> p b j", j=4), op=ALU.mult))(cm, h, r0, r1),
                             reads=[bszS], writes=[bbTm, bmixS])
                interleave([out_tile(xs_in[:, :], ys_d[:, :], lambda c, tsl: mixS[:, c, tsl], [bmixS], slice(0, NS), NS, 16)])

    P.emit()
    return nc


def _const_tables():
    slopes = np.exp2(-8.0 * (np.arange(8, dtype=np.float64) + 1.0) / 8.0)
    p = np.arange(128)[:, None]
    f = np.arange(128)[None, :]
    btab = np.zeros((4, 128, 3, 2, 512), np.float32)
    for pair in range(4):
        for ci, dil in enumerate(DILS):
            for hp in range(2):
                s = slopes[pair * 2 + hp] * dil
                da = f - p + 128
                A = np.where(f <= p, -s * da, NEG)
                db = f - p
                B = np.where(f >= p, -s * db, NEG)
                blk = np.concatenate([A, B], axis=1)
                btab[pair, :, ci, hp, :] = 8.0 * np.concatenate([blk, blk], axis=1)
    wmain = np.zeros((32, 2, 4, 128), np.float64)
    wnew = np.zeros((32, NB, NB, 4), np.float64)
    bmask = np.zeros((32, 8, 64), np.float32)
    bmaskm = np.zeros((16, 4, 64), np.float32)
    par = np.zeros((32, 2), np.float32)
    m = np.arange(128)
    for h in range(8):
        s = slopes[h]
        for j in range(4):
            pidx = h * 4 + j
            bmask[pidx, h, :] = 1.0
            par[pidx, h % 2] = 1.0
            if h < 4:
                bmaskm[pidx, h, :] = 1.0
            wmain[pidx, 0, j, :] += np.where(m < 96, np.exp(-s * 16.0 * (128 - m)), 0.0)
            for mm in range(96, 128):
                wmain[pidx, 1, j, 4 * (mm - 96)] += np.exp(-s * 16.0 * (128 - mm))
            wmain[pidx, 1, j, :] += np.exp(-s * 4.0 * (128 - m))
            for jp in range(4):
                kd = 512 + j - 4 * m - jp
                ok = (kd > j) & (kd <= 128)
                wmain[pidx, 1, jp, :] += np.where(ok, np.exp(-s * 1.0 * np.clip(kd, 0, 200)), 0.0)
            for b in range(NB):
                for jp in range(j + 1):
                    wnew[pidx, b, b, jp] += np.exp(-s * (j - jp))
                wnew[pidx, b, b, j] += 2.0
    return dict(btab=btab.reshape(4, 128, 3 * 2 * 512), slopes=slopes,
                wmain=wmain.reshape(32, 1024).astype(np.float32), wnew=wnew.reshape(32, NB * 64).astype(np.float32),
                bmask=bmask.reshape(32, 512), bmaskm=bmaskm.reshape(16, 256), par=par)


_CACHE = {}


def kernel(x_prompt, x_sample, mem_prompt, cache_win_k, cache_win_v, cache_conv, cache_mem_k, cache_mem_v,
           g_in, w_in, conv_w, g_mem, w_mem_kv, w_out, g_final):
    f32 = np.float32
    if "nc" not in _CACHE:
        _CACHE["nc"] = build_program()
        _CACHE["tabs"] = _const_tables()
    nc = _CACHE["nc"]
    tabs = _CACHE["tabs"]
    xp = np.asarray(x_prompt, f32)
    xsamp = np.asarray(x_sample, f32).reshape(128 * 4, D)
    cwk = np.asarray(cache_win_k, f32).reshape(128, 2048, 512)
    cwv = np.asarray(cache_win_v, f32).reshape(128, 2048, 512)
    ccv = np.asarray(cache_conv, f32).reshape(128, 2, 256)
    cmk = np.asarray(cache_mem_k, f32).reshape(128, 256, 256)
    cmv = np.asarray(cache_mem_v, f32).reshape(128, 256, 256)
    rep = lambda v: np.ascontiguousarray(np.broadcast_to(np.asarray(v, f32).reshape(1, D), (128, D)))
    convw = np.ascontiguousarray(np.asarray(conv_w, f32).reshape(3, 2, 128).transpose(2, 1, 0))
    ident = np.eye(128, dtype=f32)
    zeros_h = np.zeros((TH, D), f32)
    shared = dict(w_in=np.ascontiguousarray(np.asarray(w_in, f32).reshape(D, DIN)),
                  w_mem=np.ascontiguousarray(np.asarray(w_mem_kv, f32).reshape(D, 512)),
                  w_out=np.ascontiguousarray(np.asarray(w_out, f32).reshape(D, D)),
                  gin=rep(g_in), gmem=rep(g_mem), gfin=rep(g_final), convw=convw, ident=ident,
                  btab=tabs["btab"],
                  wmain=tabs["wmain"], wnew=tabs["wnew"], bmask=tabs["bmask"], bmaskm=tabs["bmaskm"], par=tabs["par"])
    in_maps = []
    for c in range(NCORES):
        b, ch = divmod(c, 4)
        m = dict(shared)
        m["xo"] = np.ascontiguousarray(xp[b, ch * T:(ch + 1) * T])
        m["xh"] = np.ascontiguousarray(xp[b, (ch - 1) * T:ch * T]) if ch > 0 else zeros_h
        m["hflag"] = np.full((128, 128), 1.0 if ch > 0 else 0.0, f32)
        m["mem"] = np.ascontiguousarray(np.asarray(mem_prompt, f32)[b])
        sl = slice(c * NB, (c + 1) * NB)
        m["xs"] = np.ascontiguousarray(xsamp[c * NS:(c + 1) * NS])
        m["cwk"] = np.ascontiguousarray(cwk[sl]); m["cwv"] = np.ascontiguousarray(cwv[sl])
        m["ccv"] = np.ascontiguousarray(ccv[sl].reshape(NB * 2, 256))
        m["cmk"] = np.ascontiguousarray(cmk[sl]); m["cmv"] = np.ascontiguousarray(cmv[sl])
        in_maps.append(m)
    res = run_bass_kernel_spmd(nc, in_maps, core_ids=list(range(NCORES)))
    R = res.results
    y_prompt = np.stack([np.concatenate([R[b * 4 + ch]["y"] for ch in range(4)], axis=0) for b in range(2)], 0)
    y_sample = np.concatenate([R[c]["ys"] for c in range(NCORES)], 0).reshape(128, 4, D)
    p_wk = np.stack([R[b * 4 + 3]["pwk"] for b in range(2)], 0).reshape(1, 2, 2048, 8, 64)
    p_wv = np.stack([R[b * 4 + 3]["pwv"] for b in range(2)], 0).reshape(1, 2, 2048, 8, 64)
    p_cv = np.stack([R[b * 4 + 3]["pconv"] for b in range(2)], 0).reshape(1, 2, 2, 256)
    p_mk = np.stack([R[b * 4]["pmk"] for b in range(2)], 0).reshape(1, 2, 256, 4, 64)
    p_mv = np.stack([R[b * 4]["pmv"] for b in range(2)], 0).reshape(1, 2, 256, 4, 64)
    s_wk = np.concatenate([R[c]["swk"] for c in range(NCORES)], 0).reshape(1, 128, 4, 8, 64)
    s_wv = np.concatenate([R[c]["swv"] for c in range(NCORES)], 0).reshape(1, 128, 4, 8, 64)
    s_cv = np.concatenate([R[c]["sconv"] for c in range(NCORES)], 0).reshape(1, 128, 2, 256)
    outs = (y_prompt, y_sample, p_wk, p_wv, p_cv, p_mk, p_mv, s_wk, s_wv, s_cv)
    return tuple(np.ascontiguousarray(o, dtype=f32) for o in outs)
```
